# Optimizing a Trainium2 kernel written in Bass

```python
import math
import jax
import jax.numpy as jnp
from jax import lax
import numpy as np


D_MODEL = 2048
BATCH = 4
SEQ = 4096
DEPTH = 2

D_MIX = D_MODEL
N_MIXERS = 4
GROUP_WIDTH = D_MIX // N_MIXERS
HEAD_DIM = 128
N_HEADS = GROUP_WIDTH // HEAD_DIM
CHUNK = 64
SHORT_CONV = 4
MIX_CHUNK = 128
CONV_WIDTH = 31
D_FF = 4 * D_MODEL
D_IN_PROJ = 12 * GROUP_WIDTH + 2 * N_HEADS
EPS = 1e-6
NEG_BIG = -1e30
TINY = 1e-30

kernel_name = 'hybrid_parallel_group_trunk'


def rms_norm(x, w):
    xf = x.astype(jnp.float32)
    y = xf * lax.rsqrt(jnp.mean(xf * xf, axis=-1, keepdims=True) + EPS)
    return (y * w.astype(jnp.float32)).astype(x.dtype)


def layer_norm(x, w, b):
    xf = x.astype(jnp.float32)
    mu = jnp.mean(xf, axis=-1, keepdims=True)
    var = jnp.mean(jnp.square(xf - mu), axis=-1, keepdims=True)
    y = (xf - mu) * lax.rsqrt(var + EPS)
    return (y * w.astype(jnp.float32) + b.astype(jnp.float32)).astype(x.dtype)


def l2_norm(x):
    return x * lax.rsqrt(jnp.sum(x * x, axis=-1, keepdims=True) + EPS)


def causal_dwconv(x, w):
    width, ch = w.shape
    return lax.conv_general_dilated(
        x, w[:, None, :].astype(x.dtype), window_strides=(1,), padding=[(width - 1, 0)],
        dimension_numbers=('NWC', 'WIO', 'NWC'), feature_group_count=ch)


def split_heads(t):
    return t.reshape(t.shape[0], t.shape[1], N_HEADS, HEAD_DIM)


def split_in_proj(p):
    gw = GROUP_WIDTH
    sizes = [gw] * 4 + [gw] * 4 + [N_HEADS, N_HEADS] + [gw] * 2 + [gw] * 2
    points, acc = [], 0
    for s in sizes[:-1]:
        acc += s
        points.append(acc)
    return jnp.split(p, points, axis=-1)


def hgrn2_recurrence(q, k, v, log_f):
    bsz, seq, nh, dk = q.shape
    dv = v.shape[-1]
    n = seq // CHUNK

    def chunks(t):
        return t.reshape(bsz, n, CHUNK, nh, t.shape[-1]).transpose(1, 0, 3, 2, 4)

    qc, kc, vc = chunks(q), chunks(k), chunks(v)
    bc = jnp.cumsum(chunks(log_f), axis=3)
    causal = jnp.tril(jnp.ones((CHUNK, CHUNK), dtype=bool))[:, :, None]

    def step(state, inp):
        q_c, k_c, v_c, b_c = inp
        rel = b_c[:, :, :, None, :] - b_c[:, :, None, :, :]
        decay = jnp.exp(jnp.where(causal, rel, NEG_BIG))
        scores = jnp.einsum('bhijd,bhjd->bhij', decay * q_c[:, :, :, None, :], k_c)
        b_end = b_c[:, :, -1:, :]
        out = (jnp.einsum('bhij,bhje->bhie', scores, v_c)
               + jnp.einsum('bhid,bhde->bhie', q_c * jnp.exp(b_c), state))
        state = (state * jnp.exp(b_end)[:, :, 0, :, None]
                 + jnp.einsum('bhjd,bhje->bhde', k_c * jnp.exp(b_end - b_c), v_c))
        return state, out

    state0 = jnp.zeros((bsz, nh, dk, dv), jnp.float32)
    _, out = lax.scan(step, state0, (qc, kc, vc, bc))
    return out.transpose(1, 0, 3, 2, 4).reshape(bsz, seq, nh, dv)


def gated_delta_rule(q, k, v, beta, g):
    bsz, seq, nh, dk = q.shape
    dv = v.shape[-1]
    n = seq // CHUNK

    def chunks(t):
        t = t.reshape((bsz, n, CHUNK, nh) + t.shape[3:])
        return jnp.moveaxis(t, 3, 1)

    qc, kc, vc = chunks(q), chunks(k), chunks(v)
    bc = chunks(beta)
    gc = jnp.cumsum(chunks(g), axis=-1)
    idx = jnp.arange(CHUNK)
    incl = idx[:, None] >= idx[None, :]
    strict = idx[:, None] > idx[None, :]
    gamma = jnp.exp(jnp.where(incl, gc[..., :, None] - gc[..., None, :], NEG_BIG))
    k_beta = kc * bc[..., None]
    m = jnp.where(strict, jnp.einsum('bhnid,bhnjd->bhnij', k_beta, kc) * gamma, 0.0)
    t_mat = m + jnp.eye(CHUNK, dtype=m.dtype)

    def solve(rhs):
        return lax.linalg.triangular_solve(t_mat, rhs, left_side=True, lower=True, unit_diagonal=True)

    u = solve(vc * bc[..., None])
    w = solve(k_beta * jnp.exp(gc)[..., None])
    qk = jnp.einsum('bhnid,bhnjd->bhnij', qc, kc) * gamma
    q_dec = qc * jnp.exp(gc)[..., None]
    k_dec = kc * jnp.exp(gc[..., -1:] - gc)[..., None]
    g_end = jnp.exp(gc[..., -1])

    def step(state, inp):
        u_c, w_c, qk_c, qd_c, kd_c, ge_c = inp
        v_new = u_c - jnp.einsum('bhik,bhkv->bhiv', w_c, state)
        out = (jnp.einsum('bhik,bhkv->bhiv', qd_c, state)
               + jnp.einsum('bhij,bhjv->bhiv', qk_c, v_new))
        state = state * ge_c[..., None, None] + jnp.einsum('bhjk,bhjv->bhkv', kd_c, v_new)
        return state, out

    xs = tuple(jnp.moveaxis(t, 2, 0) for t in (u, w, qk, q_dec, k_dec, g_end))
    _, out = lax.scan(step, jnp.zeros((bsz, nh, dk, dv), jnp.float32), xs)
    return out.transpose(1, 0, 3, 2, 4).reshape(bsz, seq, nh, dv)


def spatial_gating(u, v, ln_w, ln_b, w_s, b_s):
    bsz, seq, _ = v.shape
    n = seq // MIX_CHUNK
    v = layer_norm(v, ln_w, ln_b).reshape(bsz, n, MIX_CHUNK, N_HEADS, HEAD_DIM)
    w_causal = jnp.where(jnp.tril(jnp.ones((MIX_CHUNK, MIX_CHUNK), dtype=bool)), w_s, 0.0)
    mixed = jnp.einsum('hij,bnjhd->bnihd', w_causal, v) + b_s.T[:, :, None]
    return u * mixed.reshape(bsz, seq, GROUP_WIDTH)


def conformer_conv(a, gate, dw_w, dw_b, ln_w, ln_b):
    y = a * jax.nn.sigmoid(gate)
    y = causal_dwconv(y, dw_w) + dw_b
    return jax.nn.silu(layer_norm(y, ln_w, ln_b))


def setup_inputs(seed: int = 0) -> dict:
    key = jax.random.key(seed)
    ks = jax.random.split(key, 24)
    f32 = jnp.float32

    def normal(k, shape, scale):
        return jax.random.normal(k, shape, f32) * scale

    def gain(k, shape):
        return 1.0 + 0.02 * jax.random.normal(k, shape, f32)

    dt = jnp.exp(jax.random.uniform(ks[10], (DEPTH, N_HEADS), f32, math.log(1e-3), math.log(1e-1)))
    return {
        'x': normal(ks[0], (BATCH, SEQ, D_MODEL), 1.0),
        'lower_bounds': normal(ks[1], (DEPTH, GROUP_WIDTH), 0.1),
        'norm_mix_pre': gain(ks[2], (DEPTH, D_MODEL)),
        'norm_mix_post': gain(ks[3], (DEPTH, D_MODEL)),
        'norm_ff_pre': gain(ks[4], (DEPTH, D_MODEL)),
        'norm_ff_post': gain(ks[5], (DEPTH, D_MODEL)),
        'w_in': normal(ks[6], (DEPTH, D_MODEL, D_IN_PROJ), D_MODEL ** -0.5),
        'w_out': normal(ks[7], (DEPTH, D_MIX, D_MODEL), D_MIX ** -0.5),
        'hgrn_norm_w': gain(ks[8], (DEPTH, HEAD_DIM)),
        'gdn_conv_w': normal(ks[9], (DEPTH, SHORT_CONV, 3 * GROUP_WIDTH), SHORT_CONV ** -0.5),
        'gdn_a_log': jnp.log(jax.random.uniform(ks[11], (DEPTH, N_HEADS), f32, 1.0, 16.0)),
        'gdn_dt_bias': dt + jnp.log(-jnp.expm1(-dt)),
        'gdn_norm_w': gain(ks[12], (DEPTH, HEAD_DIM)),
        'gmlp_ln_w': gain(ks[13], (DEPTH, GROUP_WIDTH)),
        'gmlp_ln_b': normal(ks[14], (DEPTH, GROUP_WIDTH), 0.02),
        'gmlp_w_s': normal(ks[15], (DEPTH, N_HEADS, MIX_CHUNK, MIX_CHUNK), MIX_CHUNK ** -0.5),
        'gmlp_b_s': gain(ks[16], (DEPTH, N_HEADS, MIX_CHUNK)),
        'conv_dw_w': normal(ks[17], (DEPTH, CONV_WIDTH, GROUP_WIDTH), CONV_WIDTH ** -0.5),
        'conv_dw_b': normal(ks[18], (DEPTH, GROUP_WIDTH), 0.02),
        'conv_ln_w': gain(ks[19], (DEPTH, GROUP_WIDTH)),
        'conv_ln_b': normal(ks[20], (DEPTH, GROUP_WIDTH), 0.02),
        'w_ff1': normal(ks[21], (DEPTH, D_MODEL, D_FF), D_MODEL ** -0.5),
        'w_ff2': normal(ks[22], (DEPTH, D_FF, D_MODEL), D_FF ** -0.5),
    }


def reference(x, lower_bounds, norm_mix_pre, norm_mix_post, norm_ff_pre, norm_ff_post,
              w_in, w_out, hgrn_norm_w, gdn_conv_w, gdn_a_log, gdn_dt_bias, gdn_norm_w,
              gmlp_ln_w, gmlp_ln_b, gmlp_w_s, gmlp_b_s, conv_dw_w, conv_dw_b, conv_ln_w,
              conv_ln_b, w_ff1, w_ff2):
    f32 = jnp.float32
    bsz, seq, _ = x.shape
    lb_soft = jax.nn.softmax(lower_bounds.astype(f32), axis=0)
    lb_all = jnp.cumsum(lb_soft, axis=0) - lb_soft[0]

    for l in range(DEPTH):
        h = rms_norm(x, norm_mix_pre[l])
        proj = (h @ w_in[l]).astype(f32)
        (a_q, a_f, a_i, a_g, b_q, b_k, b_v, b_z, b_beta, b_a,
         c_u, c_v, d_a, d_gate) = split_in_proj(proj)

        lb = lb_all[l]
        f_a = lb + (1.0 - lb) * jax.nn.sigmoid(a_f)
        log_f = jnp.log(jnp.maximum(f_a, TINY))
        k_a = (1.0 - lb) * jax.nn.sigmoid(-a_f)
        q_a = jax.nn.silu(a_q)
        o_a = hgrn2_recurrence(split_heads(q_a), split_heads(k_a), split_heads(a_i), split_heads(log_f))
        o_a = rms_norm(o_a, hgrn_norm_w[l]).reshape(bsz, seq, GROUP_WIDTH) * jax.nn.silu(a_g)

        qkv = jax.nn.silu(causal_dwconv(jnp.concatenate([b_q, b_k, b_v], axis=-1), gdn_conv_w[l]))
        q_b, k_b, v_b = jnp.split(qkv, 3, axis=-1)
        q_b = l2_norm(split_heads(q_b)) * (HEAD_DIM ** -0.5)
        k_b = l2_norm(split_heads(k_b))
        beta = jax.nn.sigmoid(b_beta)
        g_b = -jnp.exp(gdn_a_log[l].astype(f32)) * jax.nn.softplus(b_a + gdn_dt_bias[l].astype(f32))
        o_b = gated_delta_rule(q_b, k_b, split_heads(v_b), beta, g_b)
        o_b = rms_norm(o_b, gdn_norm_w[l]).reshape(bsz, seq, GROUP_WIDTH) * jax.nn.silu(b_z)

        o_c = spatial_gating(jax.nn.gelu(c_u, approximate=False), jax.nn.gelu(c_v, approximate=False),
                             gmlp_ln_w[l], gmlp_ln_b[l], gmlp_w_s[l].astype(f32), gmlp_b_s[l].astype(f32))

        o_d = conformer_conv(d_a, d_gate, conv_dw_w[l], conv_dw_b[l], conv_ln_w[l], conv_ln_b[l])

        mix = jnp.concatenate([o_a, o_b, o_c, o_d], axis=-1).astype(x.dtype)
        x = x + rms_norm(mix @ w_out[l], norm_mix_post[l])

        h = rms_norm(x, norm_ff_pre[l])
        y = jnp.square(jax.nn.relu(h @ w_ff1[l])) @ w_ff2[l]
        x = x + rms_norm(y, norm_ff_post[l])
    return x
```

```python
from contextlib import ExitStack
import numpy as np
import concourse.bass as bass
import concourse.mybir as mybir
from concourse.bass_utils import run_bass_kernel_spmd

F32 = mybir.dt.float32
BF16 = mybir.dt.bfloat16
ALU = mybir.AluOpType
AF = mybir.ActivationFunctionType

import os
NIT = int(os.environ.get('NIT', '5'))
NT = 512
DM = 2048
NSLAB = 48
EPS = 1e-6


class Op:
    __slots__ = ("eng", "fn", "deps", "semkey", "inc", "count", "need")

    def __init__(self, eng, fn, semkey, inc):
        self.eng = eng
        self.fn = fn
        self.deps = []
        self.semkey = semkey
        self.inc = inc
        self.count = None
        self.need = False


class V:
    __slots__ = ("tl", "ap", "cells")

    def __init__(self, tl, ap, cells):
        self.tl = tl
        self.ap = ap
        self.cells = cells


class TL:
    def __init__(self, t, ncell):
        self.t = t
        self.ncell = ncell
        self.w = [None] * ncell
        self.r = [[] for _ in range(ncell)]
        self.psum = False

    def all(self):
        return V(self, self.t[:], range(self.ncell))

    def c(self, i):
        return V(self, self.t[:, i, :], (i,))

    def cs(self, i0, i1):
        return V(self, self.t[:, i0:i1, :], range(i0, i1))

    def v(self, ap, cells=None):
        return V(self, ap, range(self.ncell) if cells is None else cells)


class Prog:
    ENGS = ("pe", "act", "dve", "pool", "sp")

    def __init__(self, nc):
        self.nc = nc
        self.ops = []
        self.es = ExitStack()
        self.n_t = 0

    def sbuf(self, shape, dtype, ncell=1, name=None):
        self.n_t += 1
        t = self.es.enter_context(self.nc.sbuf_tensor(name or f"t{self.n_t}", list(shape), dtype))
        return TL(t, ncell)

    def psum(self, shape, dtype=F32, ncell=1, name=None):
        self.n_t += 1
        t = self.es.enter_context(self.nc.psum_tensor(name or f"p{self.n_t}", list(shape), dtype))
        tl = TL(t, ncell)
        tl.psum = True
        return tl

    def add(self, eng, fn, reads=(), writes=(), semkey=None, inc=1):
        op = Op(eng, fn, semkey or eng, inc)
        deps = {}
        for v in reads:
            if not isinstance(v, V):
                continue
            for c in v.cells:
                w = v.tl.w[c]
                if w is not None:
                    deps[id(w)] = w
                if v.tl.psum:
                    for r in v.tl.r[c]:
                        if r.eng != eng:
                            deps[id(r)] = r
        for v in writes:
            for c in v.cells:
                w = v.tl.w[c]
                if w is not None:
                    deps[id(w)] = w
                for r in v.tl.r[c]:
                    deps[id(r)] = r
        for v in writes:
            for c in v.cells:
                v.tl.w[c] = op
                v.tl.r[c] = []
        for v in reads:
            if not isinstance(v, V):
                continue
            for c in v.cells:
                v.tl.r[c].append(op)
        for d in deps.values():
            if d is op:
                continue
            if d.eng == "pe" and eng == "pe" and d.semkey == "pe":
                continue
            d.need = True
            op.deps.append(d)
        if semkey is not None:
            op.need = True
        self.ops.append(op)
        return op

    @staticmethod
    def _a(x):
        return x.ap if isinstance(x, V) else x

    def mm(self, out, lhsT, rhs, start=True, stop=True):
        a = self._a
        return self.add("pe", lambda e: e.matmul(a(out), a(lhsT), a(rhs), start=start, stop=stop),
                        reads=(lhsT, rhs), writes=(out,))

    def tr(self, out, in_, ident):
        a = self._a
        return self.add("pe", lambda e: e.transpose(a(out), a(in_), a(ident)),
                        reads=(in_, ident), writes=(out,))

    def act(self, out, in_, func, bias=None, scale=None, accum=None):
        a = self._a
        kw = {}
        if bias is not None:
            kw["bias"] = a(bias)
        if scale is not None:
            kw["scale"] = a(scale)
        if accum is not None:
            kw["accum_out"] = a(accum)
        w = (out,) if accum is None else (out, accum)
        return self.add("act", lambda e: e.activation(a(out), a(in_), func, **kw),
                        reads=(in_, bias, scale), writes=w)

    def tt(self, out, in0, in1, op, eng="dve"):
        a = self._a
        return self.add(eng, lambda e: e.tensor_tensor(a(out), a(in0), a(in1), op),
                        reads=(in0, in1), writes=(out,))

    def ts(self, out, in0, s1, s2, op0, op1, eng="dve"):
        a = self._a
        return self.add(eng, lambda e: e.tensor_scalar(a(out), a(in0), a(s1), a(s2), op0, op1),
                        reads=(in0, s1, s2), writes=(out,))

    def stt(self, out, in0, scalar, in1, op0, op1, eng="dve"):
        a = self._a
        return self.add(eng, lambda e: e.scalar_tensor_tensor(a(out), a(in0), a(scalar), a(in1), op0, op1),
                        reads=(in0, scalar, in1), writes=(out,))

    def copy(self, out, in_, eng="act"):
        a = self._a
        if eng == "act":
            return self.add(eng, lambda e: e.copy(a(out), a(in_)), reads=(in_,), writes=(out,))
        return self.add(eng, lambda e: e.tensor_copy(a(out), a(in_)), reads=(in_,), writes=(out,))

    def recip(self, out, in_):
        a = self._a
        return self.add("dve", lambda e: e.reciprocal(a(out), a(in_)), reads=(in_,), writes=(out,))

    def memset(self, out, val, eng="dve"):
        a = self._a
        return self.add(eng, lambda e: e.memset(a(out), val), writes=(out,))

    def scan(self, out, d0, d1, init, op0, op1):
        a = self._a
        return self.add("dve", lambda e: e.tensor_tensor_scan(a(out), a(d0), a(d1), a(init), op0, op1),
                        reads=(d0, d1, init), writes=(out,))

    def dma(self, q, out, in_, group):
        a = self._a
        r = (in_,) if isinstance(in_, V) else ()
        w = (out,) if isinstance(out, V) else ()
        return self.add(q, lambda e: e.dma_start(out=a(out), in_=a(in_)), reads=r, writes=w,
                        semkey="dma_" + group, inc=16)

    def finalize(self, final_waits=()):
        nc = self.nc
        counts = {}
        for op in self.ops:
            if op.need:
                counts[op.semkey] = counts.get(op.semkey, 0) + op.inc
                op.count = counts[op.semkey]
        sems = {}
        for k in counts:
            sems[k] = self.es.enter_context(nc.semaphore("s_" + k))
        byeng = {e: [] for e in self.ENGS}
        for op in self.ops:
            byeng[op.eng].append(op)
        self.stats = {e: len(v) for e, v in byeng.items()}
        self.counts = counts
        block = self.es.enter_context(nc.Block())

        def emit(e, name):
            waited = {}
            for op in byeng[name]:
                for d in op.deps:
                    if waited.get(d.semkey, 0) < d.count:
                        e.wait_ge(sems[d.semkey], d.count)
                        waited[d.semkey] = d.count
                ins = op.fn(e)
                if op.need:
                    ins.then_inc(sems[op.semkey], op.inc)
            if name == "sp":
                for k in final_waits:
                    kk = "dma_" + k
                    if kk in counts:
                        e.wait_ge(sems[kk], counts[kk])

        @block.tensor
        def _(e):
            emit(e, "pe")

        @block.scalar
        def _(e):
            emit(e, "act")

        @block.vector
        def _(e):
            emit(e, "dve")

        @block.gpsimd
        def _(e):
            emit(e, "pool")

        @block.sync
        def _(e):
            emit(e, "sp")

        self.es.close()


PP_FIELDS = [("nmp", 16), ("nmo", 16), ("nfp", 16), ("nfo", 16), ("lb0", 4), ("lb1", 4), ("hgw", 1), ("gdw", 1),
             ("gconv", 48), ("cvw", 124), ("cvb", 4), ("clw", 4), ("clb", 4), ("glw", 4), ("glb", 4),
             ("alog", 4), ("dtb", 4)]
PP_OFF = {}
_o = 0
for _n, _w in PP_FIELDS:
    PP_OFF[_n] = (_o, _w)
    _o += _w
PP_W = _o
CST_NAMES = ["ident", "ut", "sl", "ec", "mu128", "ones"]


def chunked(v):
    return np.ascontiguousarray(v.reshape(-1, 128).T)


def pack_params(inp, depth):
    pp = np.zeros((128, depth, PP_W), np.float32)
    for l in range(depth):
        def put(name, arr):
            o, w = PP_OFF[name]
            pp[:, l, o:o + w] = arr
        put("nmp", chunked(inp["norm_mix_pre"][l]))
        put("nmo", chunked(inp["norm_mix_post"][l]))
        put("nfp", chunked(inp["norm_ff_pre"][l]))
        put("nfo", chunked(inp["norm_ff_post"][l]))
        put("lb0", chunked(inp["lower_bounds"][0]))
        put("lb1", chunked(inp["lower_bounds"][1]))
        put("hgw", inp["hgrn_norm_w"][l].reshape(128, 1))
        put("gdw", inp["gdn_norm_w"][l].reshape(128, 1))
        gc = inp["gdn_conv_w"][l]
        put("gconv", np.concatenate([chunked(gc[k]) for k in range(4)], axis=1))
        cw = inp["conv_dw_w"][l]
        put("cvw", np.concatenate([chunked(cw[k]) for k in range(31)], axis=1))
        put("cvb", chunked(inp["conv_dw_b"][l]))
        put("clw", chunked(inp["conv_ln_w"][l]))
        put("clb", chunked(inp["conv_ln_b"][l]))
        put("glw", chunked(inp["gmlp_ln_w"][l]))
        put("glb", chunked(inp["gmlp_ln_b"][l]))
        put("alog", np.broadcast_to(inp["gdn_a_log"][l][None, :], (128, 4)))
        put("dtb", np.broadcast_to(inp["gdn_dt_bias"][l][None, :], (128, 4)))
    return pp.reshape(128, depth * PP_W)


def make_consts():
    i = np.arange(128)
    same = (i[:, None] // 64) == (i[None, :] // 64)
    c = {
        "ident": np.eye(128),
        "ut": (i[:, None] <= i[None, :]) & same,
        "sl": (i[:, None] > i[None, :]) & same,
        "ec": same,
        "mu128": (i[:, None] <= i[None, :]),
        "ones": np.ones((128, 128)),
    }
    return np.concatenate([c[n].astype(np.float32) for n in CST_NAMES], axis=1)


IN_SLAB_COLS = [0, 512, 1024, 1536, 2048, 2560, 3072, 3584, 4104, 4616, 5128, 5640]


def slabify(w):
    return w.reshape(16, 128, 512).transpose(1, 0, 2).reshape(128, 8192)


def pack_weights(inp, depth):
    wsl = np.empty((depth * NSLAB, 128, 8192), np.float32)
    wsm = np.empty((depth, 128, 128), np.float32)
    for l in range(depth):
        b = l * NSLAB
        w_in = inp["w_in"][l]
        for s, c0 in enumerate(IN_SLAB_COLS):
            wsl[b + s] = slabify(w_in[:, c0:c0 + 512])
        wsm[l] = w_in[:, 4096:4104].reshape(16, 128, 8).transpose(1, 0, 2).reshape(128, 128)
        for s in range(4):
            wsl[b + 12 + s] = slabify(inp["w_out"][l][:, s * 512:(s + 1) * 512])
        for s in range(16):
            wsl[b + 16 + s] = slabify(inp["w_ff1"][l][:, s * 512:(s + 1) * 512])
        w2 = inp["w_ff2"][l]
        for nb in range(4):
            for g in range(4):
                wsl[b + 32 + nb * 4 + g] = slabify(w2[g * 2048:(g + 1) * 2048, nb * 512:(nb + 1) * 512])
    return wsl, wsm


def build(T, depth, dbg=None, mixers="ABCD", bstage=9):
    nc = bass.Bass("TRN2", target_bir_lowering=False)
    ntile = T // NT
    xT_d = nc.dram_tensor("xT", [DM, T], F32, kind="ExternalInput").ap()
    wsl_d = nc.dram_tensor("wsl", [depth * NSLAB, 128, 8192], F32, kind="ExternalInput").ap()
    wsm_d = nc.dram_tensor("wsm", [depth, 128, 128], F32, kind="ExternalInput").ap()
    pp_d = nc.dram_tensor("pp", [128, depth * PP_W], F32, kind="ExternalInput").ap()
    cst_d = nc.dram_tensor("cst", [128, 128 * len(CST_NAMES)], F32, kind="ExternalInput").ap()
    ws_d = nc.dram_tensor("gws", [depth, 128, 512], F32, kind="ExternalInput").ap()
    bs_d = nc.dram_tensor("gbs", [depth, 128, 512], F32, kind="ExternalInput").ap()
    yT_d = nc.dram_tensor("yT", [DM, T], F32, kind="ExternalOutput").ap()
    dbg_d = None
    if dbg:
        dbg_d = nc.dram_tensor("dbg", [DM, T], F32, kind="ExternalOutput").ap()

    P = Prog(nc)
    X = P.sbuf([128, 16, NT], F32, 16, "X")
    HT = P.sbuf([128, 16, NT], BF16, 16, "HT")
    MIXT = P.sbuf([128, 16, NT], BF16, 16, "MIXT")
    WIDE = [P.sbuf([128, 4, NT], F32, 4, f"WIDE{i}") for i in range(3)]
    SS = [P.sbuf([128, NT], F32, 2, f"S{i}") for i in range(28)]
    SLAB = [P.sbuf([128, 16, 512], BF16, 1, f"SLAB{i}") for i in range(2)]
    CB = [P.sbuf([128, 544], F32, 1, f"CB{i}") for i in range(2)]
    CBQ = TL(CB[0].t, 2)
    PPT = P.sbuf([128, depth * PP_W], F32, 1, "PPT")
    CST = P.sbuf([128, 128 * len(CST_NAMES)], F32, 1, "CST")
    WSM = P.sbuf([128, depth, 16, 8], BF16, 1, "WSM")
    ONEB = P.sbuf([128, 128], BF16, 1, "ONEB")
    ONEF = P.sbuf([128, NT], F32, 1, "ONEF")
    GWS = P.sbuf([128, depth, 4, 128], F32, 1, "GWS")
    R2 = P.sbuf([128, depth, 4, 128], F32, 1, "R2")
    LB = P.sbuf([128, depth, 4], F32, 1, "LB")
    OML = P.sbuf([128, depth, 4], F32, 1, "OML")
    NOML = P.sbuf([128, depth, 4], F32, 1, "NOML")
    NEGA = P.sbuf([128, depth, 4], F32, 1, "NEGA")
    SA = P.sbuf([128, depth, 4, 128], F32, depth, "SA")
    SB = P.sbuf([128, depth, 4, 128], F32, depth, "SB")
    HD = P.sbuf([128, depth, 4, 30], F32, depth, "HD")
    HG = P.sbuf([128, depth, 12, 3], F32, depth, "HG")
    SM = [P.sbuf([128, 32], F32, 1, f"SM{i}") for i in range(16)]
    PS = [P.psum([128, 512], F32, 1, f"PS{i}") for i in range(8)]
    st = {"ps": 0, "sm": 0}

    PN = PS[0]
    PO = PS[1]

    def ps():
        st["ps"] = (st["ps"] + 1) % 6
        return PS[2 + st["ps"]]

    def sm():
        st["sm"] = (st["sm"] + 1) % 14
        return SM[st["sm"]]

    def cst(name):
        i = CST_NAMES.index(name)
        return CST.v(CST.t[:, i * 128:(i + 1) * 128])

    def pp(l, name, c=None):
        o, w = PP_OFF[name]
        o += l * PP_W
        if c is None:
            return PPT.v(PPT.t[:, o:o + w])
        return PPT.v(PPT.t[:, o + c:o + c + 1])

    free = list(range(28))

    def salloc():
        return SS[free.pop(0)]

    def sfree(*ts):
        for t in ts:
            free.append(SS.index(t))

    flags = {"A_done": True}

    def drive(gens):
        gens = list(gens)
        while gens:
            for g_ in list(gens):
                try:
                    next(g_)
                except StopIteration:
                    gens.remove(g_)

    def h3(v_ap):
        return v_ap.rearrange("p (h i) -> p h i", h=4)

    P.dma("sp", PPT.all(), pp_d, "c0")
    P.dma("sp", CST.all(), cst_d, "c1")
    P.dma("pool", WSM.v(WSM.t[:].rearrange("p l k n -> p l (k n)")), wsm_d.rearrange("l p n -> p l n"), "c2")
    P.memset(ONEB.all(), 1.0)
    P.memset(ONEF.all(), 1.0)
    P.memset(SA.all(), 0.0)
    P.memset(SB.all(), 0.0)
    P.memset(HD.all(), 0.0)
    P.memset(HG.all(), 0.0)
    P.memset(LB.all(), 0.0)
    for l in range(1, depth):
        t = sm()
        P.tt(t.v(t.t[:, 0:4]), pp(0, "lb1"), pp(0, "lb0"), ALU.subtract)
        P.act(LB.v(LB.t[:, l, :]), t.v(t.t[:, 0:4]), AF.Sigmoid)
    P.ts(OML.all(), LB.all(), -1.0, 1.0, ALU.mult, ALU.add)
    P.ts(NOML.all(), LB.all(), 1.0, -1.0, ALU.mult, ALU.add)
    for l in range(depth):
        P.act(NEGA.v(NEGA.t[:, l, :]), pp(l, "alog"), AF.Exp)
    P.ts(NEGA.all(), NEGA.all(), -1.0, 0.0, ALU.mult, ALU.add)
    for l in range(depth):
        tw = salloc()
        tb = salloc()
        P.dma("sp", tw.all(), ws_d[l], f"c3_{l}")
        P.dma("sp", tb.all(), bs_d[l], f"c4_{l}")
        P.tt(GWS.v(GWS.t[:, l, :, :]), tw.v(h3(tw.t[:])), CST.v(cst("mu128").ap.unsqueeze(1).to_broadcast([128, 4, 128])),
             ALU.mult)
        pr = ps()
        P.mm(pr.all(), cst("ones"), GWS.v(GWS.t[:, l, :, :].rearrange("p h i -> p (h i)")))
        for h in range(4):
            P.stt(R2.v(R2.t[:, l, h, :]), pr.v(pr.t[:, h * 128:(h + 1) * 128]), pp(l, "glb", h),
                  tb.v(tb.t[:, h * 128:(h + 1) * 128]), ALU.mult, ALU.add)
        sfree(tw, tb)

    slab_seq = [(ti, l, s) for ti in range(ntile) for l in range(depth) for s in range(NSLAB)]
    wst = {"issued": 0, "used": 0}

    def issue_slab():
        i = wst["issued"]
        if i >= len(slab_seq):
            return
        _, l, s = slab_seq[i]
        buf = SLAB[i % 2]
        P.dma("pool", buf.v(buf.t[:].rearrange("p k n -> p (k n)")), wsl_d[l * NSLAB + s], f"w{i % 2}")
        wst["issued"] += 1

    def next_slab(l, s):
        i = wst["used"]
        assert slab_seq[i][1] == l and slab_seq[i][2] == s, (slab_seq[i], l, s)
        while wst["issued"] < min(i + 2, len(slab_seq)):
            issue_slab()
        wst["used"] += 1
        return SLAB[i % 2]

    def proj_F(slab, j, rhs_chunks, out_v, first=True, last=True):
        for k in range(16):
            P.mm(out_v, slab.v(slab.t[:, k, j * 128:(j + 1) * 128]), rhs_chunks(k),
                 start=(first and k == 0), stop=(last and k == 15))

    def proj_T(slab, sub, out_v):
        for k in range(16):
            P.mm(out_v, HT.v(HT.t[:, k, sub * 128:(sub + 1) * 128], (k,)), slab.v(slab.t[:, k, :]),
                 start=(k == 0), stop=(k == 15))

    def rstd_from_ms(ms_v, out_v, scale):
        P.act(out_v, ms_v, AF.Sqrt, bias=EPS, scale=scale)
        P.recip(out_v, out_v)

    def prenorm(l, wname):
        sq = salloc()
        rs = salloc()
        pm = PN
        for c in range(16):
            P.act(sq.v(sq.t[:].bitcast(BF16)[:, 0:NT], (0,)), X.c(c), AF.Square)
            P.mm(pm.all(), ONEB.all(), sq.v(sq.t[:].bitcast(BF16)[:, 0:NT], (0,)), start=(c == 0), stop=(c == 15))
        rstd_from_ms(pm.all(), rs.all(), 1.0 / DM)
        for c in range(16):
            P.stt(HT.c(c), X.c(c), pp(l, wname, c), rs.all(), ALU.mult, ALU.mult)
        sfree(sq, rs)

    def postnorm_add(l, wname, Y):
        sq = salloc()
        rs = salloc()
        pm = ps()
        for c in range(16):
            P.act(sq.v(sq.t[:].bitcast(BF16)[:, 0:NT], (0,)), Y[c].all(), AF.Square)
            P.mm(pm.all(), ONEB.all(), sq.v(sq.t[:].bitcast(BF16)[:, 0:NT], (0,)), start=(c == 0), stop=(c == 15))
        rstd_from_ms(pm.all(), rs.all(), 1.0 / DM)
        for c in range(16):
            P.stt(Y[c].all(), Y[c].all(), pp(l, wname, c), rs.all(), ALU.mult, ALU.mult)
            P.tt(X.c(c), X.c(c), Y[c].all(), ALU.add)
        sfree(sq, rs)

    nst = {"i": 0}

    def evac_y(l, wname, c, pz, Yc, sqs):
        hf = nst["i"] % 2
        nst["i"] += 1
        sqv = sqs.v(sqs.t[:].bitcast(BF16)[:, hf * NT:(hf + 1) * NT], (hf,))
        P.act(sqv, pz.all(), AF.Square)
        P.mm(PN.all(), ONEB.all(), sqv, start=(c == 0), stop=(c == 15))
        P.act(Yc.all(), pz.all(), AF.Copy, scale=pp(l, wname, c))

    def post_pre(Y, sqs, lpre, wpre):
        rs = salloc()
        rstd_from_ms(PN.all(), rs.all(), 1.0 / DM)
        for c in range(16):
            P.tt(Y[c].all(), Y[c].all(), rs.all(), ALU.mult, eng="pool")
            P.tt(X.c(c), X.c(c), Y[c].all(), ALU.add)
            if wpre is not None:
                hf = c % 2
                sqv = sqs.v(sqs.t[:].bitcast(BF16)[:, hf * NT:(hf + 1) * NT], (hf,))
                P.act(sqv, X.c(c), AF.Square)
                P.mm(PN.all(), ONEB.all(), sqv, start=(c == 0), stop=(c == 15))
        if wpre is not None:
            rstd_from_ms(PN.all(), rs.all(), 1.0 / DM)
            for c in range(16):
                P.stt(HT.c(c), X.c(c), pp(lpre, wpre, c), rs.all(), ALU.mult, ALU.mult)
        sfree(rs)

    def head_epilogue(l, po, gate, s, mix0, wname):
        sq = salloc()
        rs = salloc()
        P.act(sq.all(), po.all(), AF.Square)
        pm = ps()
        P.mm(pm.all(), cst("ones"), sq.all())
        rstd_from_ms(pm.all(), rs.all(), 1.0 / 128)
        P.tt(sq.all(), po.all(), rs.all(), ALU.mult)
        P.stt(MIXT.v(MIXT.t[:, mix0:mix0 + 4, s * 128:(s + 1) * 128], range(mix0, mix0 + 4)), sq.v(h3(sq.t[:])),
              pp(l, wname), gate.v(gate.t[:, :, s * 128:(s + 1) * 128]), ALU.mult, ALU.mult)
        sfree(sq, rs)

    hT_chunk = lambda k: HT.c(k)

    def mixer_D(l):
        sa = next_slab(l, 10)
        pa = [ps() for _ in range(4)]
        for c in range(4):
            proj_F(sa, c, hT_chunk, pa[c].all())
        ya = [salloc() for _ in range(4)]
        for c in range(4):
            P.copy(ya[c].all(), pa[c].all())
        sg = next_slab(l, 11)
        acc = [salloc() for _ in range(4)]
        for c in range(4):
            pg = ps()
            proj_F(sg, c, hT_chunk, pg.all())
            P.act(acc[c].all(), pg.all(), AF.Sigmoid)
        for c in range(4):
            cb = CB[c % 2]
            sgm = acc[c]
            P.copy(cb.v(cb.t[:, 0:30]), HD.v(HD.t[:, l, c, :], (l,)), eng="dve")
            P.tt(cb.v(cb.t[:, 30:542]), ya[c].all(), sgm.all(), ALU.mult)
            P.copy(HD.v(HD.t[:, l, c, :], (l,)), cb.v(cb.t[:, 512:542]), eng="dve")
            o, _ = PP_OFF["cvw"]
            P.ts(acc[c].all(), cb.v(cb.t[:, 0:512]), pp(l, "cvw", 0 * 4 + c), pp(l, "cvb", c), ALU.mult, ALU.add)
            for k in range(1, 31):
                P.stt(acc[c].all(), cb.v(cb.t[:, k:k + 512]), pp(l, "cvw", k * 4 + c), acc[c].all(), ALU.mult, ALU.add)
        pmean = ps()
        pex2 = ps()
        for c in range(4):
            P.mm(pmean.all(), cst("ones"), acc[c].all(), start=(c == 0), stop=(c == 3))
        for c in range(4):
            P.act(ya[c].all(), acc[c].all(), AF.Square)
            P.mm(pex2.all(), cst("ones"), ya[c].all(), start=(c == 0), stop=(c == 3))
        mean = ya[0]
        rs = ya[1]
        var = ya[2]
        P.act(mean.all(), pmean.all(), AF.Copy, scale=1.0 / 512)
        P.tt(var.all(), mean.all(), mean.all(), ALU.mult)
        P.stt(var.all(), pex2.all(), 1.0 / 512, var.all(), ALU.mult, ALU.subtract)
        rstd_from_ms(var.all(), rs.all(), 1.0)
        for c in range(4):
            P.tt(acc[c].all(), acc[c].all(), mean.all(), ALU.subtract)
            P.tt(acc[c].all(), acc[c].all(), rs.all(), ALU.mult)
            P.act(MIXT.c(12 + c), acc[c].all(), AF.Silu, bias=pp(l, "clb", c), scale=pp(l, "clw", c))
        sfree(*ya, *acc)

    def mixer_C(l):
        su = next_slab(l, 8)
        gu = [salloc() for _ in range(4)]
        for c in range(4):
            pu = ps()
            proj_F(su, c, hT_chunk, pu.all())
            P.act(gu[c].all(), pu.all(), AF.Gelu)
        sv = next_slab(l, 9)
        vn = [salloc() for _ in range(4)]
        for s in range(4):
            pv = ps()
            proj_T(sv, s, pv.all())
            t1, t2, t3 = sm(), sm(), sm()
            sq = salloc()
            P.memset(t1.v(t1.t[:, 0:1]), 0.0)
            P.memset(t2.v(t2.t[:, 0:1]), 0.0)
            P.act(vn[s].all(), pv.all(), AF.Gelu, accum=t1.v(t1.t[:, 0:1]))
            P.act(sq.all(), vn[s].all(), AF.Square, accum=t2.v(t2.t[:, 0:1]))
            sfree(sq)
            P.ts(t1.v(t1.t[:, 1:2]), t1.v(t1.t[:, 0:1]), 1.0 / 512, 0.0, ALU.mult, ALU.add)
            P.tt(t3.v(t3.t[:, 0:1]), t1.v(t1.t[:, 1:2]), t1.v(t1.t[:, 1:2]), ALU.mult)
            P.stt(t3.v(t3.t[:, 1:2]), t2.v(t2.t[:, 0:1]), 1.0 / 512, t3.v(t3.t[:, 0:1]), ALU.mult, ALU.subtract)
            rstd_from_ms(t3.v(t3.t[:, 1:2]), t3.v(t3.t[:, 2:3]), 1.0)
            P.ts(vn[s].all(), vn[s].all(), t1.v(t1.t[:, 1:2]), t3.v(t3.t[:, 2:3]), ALU.subtract, ALU.mult)
        for h in range(4):
            pmx = ps()
            for s in range(4):
                P.mm(pmx.v(pmx.t[:, s * 128:(s + 1) * 128]), vn[s].v(vn[s].t[:, h * 128:(h + 1) * 128]),
                     GWS.v(GWS.t[:, l, h, :]))
            tmp = salloc()
            P.stt(tmp.v(h3(tmp.t[:])), pmx.v(h3(pmx.t[:])), pp(l, "glw", h),
                  R2.v(R2.t[:, l, h, :].unsqueeze(1).to_broadcast([128, 4, 128])), ALU.mult, ALU.add)
            P.tt(MIXT.c(8 + h), tmp.all(), gu[h].all(), ALU.mult)
            sfree(tmp)
        sfree(*gu, *vn)

    def mixer_A(l):
        gate = WIDE[0]
        sq_ = next_slab(l, 0)
        qe = [salloc() for _ in range(4)]
        ke = [salloc() for _ in range(4)]
        keT = [salloc() for _ in range(4)]
        SC = salloc()
        scv = SC.t[:].rearrange("p (h k c) -> p h k c", h=4, k=16)
        for h in range(4):
            pq = ps()
            proj_F(sq_, h, hT_chunk, pq.all())
            P.act(qe[h].all(), pq.all(), AF.Silu)
        sf = next_slab(l, 1)
        sigs = []
        for h in range(4):
            pf = ps()
            proj_F(sf, h, hT_chunk, pf.all())
            sig = salloc()
            P.act(sig.all(), pf.all(), AF.Sigmoid)
            sigs.append(sig)
        sv = next_slab(l, 2)
        vT = [salloc() for _ in range(4)]
        for s in range(4):
            pv = ps()
            proj_T(sv, s, pv.all())
            P.copy(vT[s].all(), pv.all())
        sg = next_slab(l, 3)
        for h in range(4):
            pg = ps()
            proj_F(sg, h, hT_chunk, pg.all())
            P.act(gate.c(h), pg.all(), AF.Silu)
        for h in range(4):
            sig = sigs[h]
            lf, B = salloc(), salloc()
            P.ts(lf.all(), sig.all(), OML.v(OML.t[:, l, h:h + 1]), LB.v(LB.t[:, l, h:h + 1]), ALU.mult, ALU.add)
            P.act(lf.all(), lf.all(), AF.Ln)
            P.ts(ke[h].all(), sig.all(), NOML.v(NOML.t[:, l, h:h + 1]), OML.v(OML.t[:, l, h:h + 1]), ALU.mult,
                 ALU.add)
            P.scan(B.all(), ONEF.all(), lf.all(), 0.0, ALU.mult, ALU.add)
            B3 = B.t[:].rearrange("p (c t) -> p c t", t=64)
            bs_ = sm()
            P.memset(bs_.v(bs_.t[:, 0:1]), 0.0)
            P.copy(bs_.v(bs_.t[:, 1:8]), B.v(B3[:, 0:7, 63]), eng="dve")
            bm, be = sm(), sm()
            P.tt(bm.v(bm.t[:, 0:8]), B.v(B3[:, :, 31]), bs_.v(bs_.t[:, 0:8]), ALU.subtract)
            P.tt(be.v(be.t[:, 0:8]), B.v(B3[:, :, 63]), bs_.v(bs_.t[:, 0:8]), ALU.subtract)
            P.act(SC.v(scv[:, h, 0, :]), bm.v(bm.t[:, 0:8]), AF.Exp)
            P.act(SC.v(scv[:, h, 1, :]), be.v(be.t[:, 0:8]), AF.Exp)
            P.tt(be.v(be.t[:, 0:8]), be.v(be.t[:, 0:8]), bm.v(bm.t[:, 0:8]), ALU.subtract)
            P.act(SC.v(scv[:, h, 2, :]), be.v(be.t[:, 0:8]), AF.Exp)
            bp = lf
            P.tt(bp.v(bp.t[:].rearrange("p (c t) -> p c t", t=64)), B.v(B3),
                 B.v(B3[:, :, 31:32].to_broadcast([128, 8, 64])), ALU.subtract)
            P.act(sig.all(), bp.all(), AF.Exp)
            P.tt(qe[h].all(), qe[h].all(), sig.all(), ALU.mult)
            P.act(sig.all(), bp.all(), AF.Exp, scale=-1.0)
            P.tt(ke[h].all(), ke[h].all(), sig.all(), ALU.mult)
            sfree(sig, lf, B)
        for h in range(4):
            pt = ps()
            for s in range(4):
                P.tr(pt.v(pt.t[:, s * 128:(s + 1) * 128]), ke[h].v(ke[h].t[:, s * 128:(s + 1) * 128]), cst("ident"))
            P.copy(keT[h].all(), pt.all())
        Sp = salloc()
        tmp = salloc()
        smk = salloc()
        S_v = SA.v(SA.t[:, l, :, :], (l,))

        def scb(k, cc):
            return SC.v(scv[:, :, k, cc:cc + 1].to_broadcast([128, 4, 128]))

        for s in range(4):
            pS = ps()
            for h in range(4):
                P.mm(pS.v(pS.t[:, h * 128:(h + 1) * 128]), ke[h].v(ke[h].t[:, s * 128:(s + 1) * 128]),
                     qe[h].v(qe[h].t[:, s * 128:(s + 1) * 128]))
            P.tt(smk.v(h3(smk.t[:])), pS.v(h3(pS.t[:])), CST.v(cst("ut").ap.unsqueeze(1).to_broadcast([128, 4, 128])),
                 ALU.mult)
            yield
            po = PO
            for cl in range(2):
                cc = s * 2 + cl
                P.tt(Sp.v(h3(Sp.t[:])), S_v, scb(0, cc), ALU.mult)
                for h in range(4):
                    c0 = h * 128 + cl * 64
                    P.mm(po.v(po.t[:, c0:c0 + 64]), vT[s].v(vT[s].t[:, h * 128:(h + 1) * 128]),
                         smk.v(smk.t[:, c0:c0 + 64]), start=True, stop=False)
                    P.mm(po.v(po.t[:, c0:c0 + 64]), Sp.v(Sp.t[:, h * 128:(h + 1) * 128]),
                         qe[h].v(qe[h].t[:, cc * 64:(cc + 1) * 64]), start=False, stop=True)
                yield
                pU = ps()
                for h in range(4):
                    P.mm(pU.v(pU.t[:, h * 128:(h + 1) * 128]),
                         keT[h].v(keT[h].t[cl * 64:(cl + 1) * 64, s * 128:(s + 1) * 128]),
                         vT[s].v(vT[s].t[cl * 64:(cl + 1) * 64, h * 128:(h + 1) * 128]))
                P.tt(S_v, S_v, scb(1, cc), ALU.mult)
                P.tt(tmp.v(h3(tmp.t[:])), pU.v(h3(pU.t[:])), scb(2, cc), ALU.mult)
                P.tt(S_v, S_v, tmp.v(h3(tmp.t[:])), ALU.add)
                yield
            head_epilogue(l, po, gate, s, 0, "hgw")
            yield
        sfree(*qe, *ke, *keT, *vT, SC, Sp, tmp, smk)
        flags["A_done"] = True

    def mixer_B(l):
        QN, KN, gate = WIDE[1], WIDE[2], WIDE[0]
        VC = [salloc() for _ in range(4)]
        dsts = [lambda h: QN.c(h), lambda h: KN.c(h), lambda h: VC[h].all()]
        for which in range(3):
            sl = next_slab(l, 4 + which)
            for h in range(4):
                ch = which * 4 + h
                pq = ps()
                proj_F(sl, h, hT_chunk, pq.all())
                cb = CB[h % 2]
                P.copy(cb.v(cb.t[:, 0:3]), HG.v(HG.t[:, l, ch, :], (l,)), eng="dve")
                P.copy(cb.v(cb.t[:, 3:515]), pq.all())
                P.copy(HG.v(HG.t[:, l, ch, :], (l,)), cb.v(cb.t[:, 512:515]), eng="dve")
                acc = salloc()
                P.ts(acc.all(), cb.v(cb.t[:, 0:512]), pp(l, "gconv", 0 * 12 + ch), 0.0, ALU.mult, ALU.add)
                for k in range(1, 4):
                    P.stt(acc.all(), cb.v(cb.t[:, k:k + 512]), pp(l, "gconv", k * 12 + ch), acc.all(), ALU.mult,
                          ALU.add)
                P.act(dsts[which](h), acc.all(), AF.Silu)
                sfree(acc)
                yield
        while not flags["A_done"]:
            yield
        for which, W in enumerate((QN, KN)):
            for h in range(4):
                sq, rs = salloc(), salloc()
                P.act(sq.all(), W.c(h), AF.Square)
                pm = ps()
                P.mm(pm.all(), cst("ones"), sq.all())
                rstd_from_ms(pm.all(), rs.all(), 1.0)
                if which == 0:
                    P.stt(W.c(h), W.c(h), 128 ** -0.5, rs.all(), ALU.mult, ALU.mult)
                else:
                    P.tt(W.c(h), W.c(h), rs.all(), ALU.mult)
                sfree(sq, rs)
        sz = next_slab(l, 7)
        for h in range(4):
            pg = ps()
            proj_F(sz, h, hT_chunk, pg.all())
            P.act(gate.c(h), pg.all(), AF.Silu)
        pbg = ps()
        for s in range(4):
            for k in range(16):
                P.mm(pbg.v(pbg.t[:, s * 8:(s + 1) * 8]), HT.v(HT.t[:, k, s * 128:(s + 1) * 128], (k,)),
                     WSM.v(WSM.t[:, l, k, :]), start=(k == 0), stop=(k == 15))
        BG = sm()
        bg3 = BG.t[:, 0:32].rearrange("p (s n) -> p s n", n=8)
        pbg3 = pbg.t[:, 0:32].rearrange("p (s n) -> p s n", n=8)
        tg = sm()
        tg3 = tg.t[:, 0:16].rearrange("p (s n) -> p s n", n=4)
        P.act(BG.v(bg3[:, :, 0:4]), pbg.v(pbg3[:, :, 0:4]), AF.Sigmoid)
        P.tt(tg.v(tg3), pbg.v(pbg3[:, :, 4:8]), PPT.v(pp(l, "dtb").ap.unsqueeze(1).to_broadcast([128, 4, 4])), ALU.add)
        P.act(tg.v(tg3), tg.v(tg3), AF.Exp)
        P.act(tg.v(tg3), tg.v(tg3), AF.Ln, bias=1.0)
        P.tt(BG.v(bg3[:, :, 4:8]), tg.v(tg3), NEGA.v(NEGA.t[:, l, :].unsqueeze(1).to_broadcast([128, 4, 4])), ALU.mult)

        if bstage <= 1:
            P.memset(MIXT.cs(4, 8), 0.0)
            sfree(*VC)
            return
        Dall, G, GT, nbM, Rv, Rk, U, WT, Ebc, QD, KD, VN, DallB = [salloc() for _ in range(13)]
        PT2 = [salloc(), salloc()]
        QK2 = [salloc(), salloc()]
        Nn = [salloc(), salloc()]
        Yy = [salloc(), salloc()]
        SC1 = [SM[14], SM[15]]
        S_v = SB.v(SB.t[:, l, :, :], (l,))
        bc_h = lambda name: CST.v(cst(name).ap.unsqueeze(1).to_broadcast([128, 4, 128]))

        def front(s):
            PT, qkm, sc1 = PT2[s % 2], QK2[s % 2], SC1[s % 2]
            beta = BG.v(bg3[:, s, 0:4])
            g = BG.v(bg3[:, s, 4:8])
            gb = BG.v(bg3[:, s, 4:8].unsqueeze(2).to_broadcast([128, 4, 128]))
            pg2 = ps()
            P.mm(pg2.v(pg2.t[:, 0:4]), cst("ut"), g)
            P.mm(pg2.v(pg2.t[:, 4:8]), cst("ec"), g)
            gcs = sm()
            P.copy(gcs.v(gcs.t[:, 0:8]), pg2.v(pg2.t[:, 0:8]))
            yield
            P.act(sc1.v(sc1.t[:, 0:4]), gcs.v(gcs.t[:, 0:4]), AF.Exp)
            P.tt(sc1.v(sc1.t[:, 4:8]), sc1.v(sc1.t[:, 0:4]), beta, ALU.mult)
            P.tt(sc1.v(sc1.t[:, 8:12]), gcs.v(gcs.t[:, 4:8]), gcs.v(gcs.t[:, 0:4]), ALU.subtract)
            P.act(sc1.v(sc1.t[:, 8:12]), sc1.v(sc1.t[:, 8:12]), AF.Exp)
            P.ts(sc1.v(sc1.t[:, 12:16]), beta, -1.0, 0.0, ALU.mult, ALU.add)
            sbc = lambda a0: sc1.v(sc1.t[:, a0:a0 + 4].unsqueeze(2).to_broadcast([128, 4, 128]))
            P.tt(Dall.v(h3(Dall.t[:])), bc_h("sl"), gb, ALU.mult)
            yield
            pr = ps()
            P.mm(pr.all(), cst("ut"), Dall.all())
            prT = ps()
            for h in range(4):
                P.mm(prT.v(prT.t[:, h * 128:(h + 1) * 128]), Dall.v(Dall.t[:, h * 128:(h + 1) * 128]), cst("ut"))
            yield
            P.act(G.all(), pr.all(), AF.Exp)
            P.act(GT.all(), prT.all(), AF.Exp)
            P.tt(GT.v(h3(GT.t[:])), GT.v(h3(GT.t[:])), bc_h("ut"), ALU.mult)
            P.tt(nbM.v(h3(nbM.t[:])), bc_h("sl"), sbc(12), ALU.mult)
            yield
            pA = ps()
            pQ = ps()
            for h in range(4):
                kcols = KN.v(KN.t[:, h, s * 128:(s + 1) * 128], (h,))
                P.mm(pA.v(pA.t[:, h * 128:(h + 1) * 128]), kcols, kcols)
                P.mm(pQ.v(pQ.t[:, h * 128:(h + 1) * 128]), kcols, QN.v(QN.t[:, h, s * 128:(s + 1) * 128], (h,)))
            yield
            N0, Y0 = Nn[0], Yy[0]
            P.tt(N0.all(), pA.all(), nbM.all(), ALU.mult)
            P.tt(N0.all(), N0.all(), G.all(), ALU.mult)
            P.tt(qkm.all(), pQ.all(), GT.all(), ALU.mult)
            yield
            pY = ps()
            for h in range(4):
                P.mm(pY.v(pY.t[:, h * 128:(h + 1) * 128]), N0.v(N0.t[:, h * 128:(h + 1) * 128]), cst("ident"))
            P.copy(Y0.all(), pY.all())
            P.tt(PT.v(h3(PT.t[:])), Y0.v(h3(Y0.t[:])), bc_h("ident"), ALU.add)
            yield
            cur = 0
            for it in range(NIT):
                Nc, Yc, Nx, Yx = Nn[cur], Yy[cur], Nn[1 - cur], Yy[1 - cur]
                pN = ps()
                for h in range(4):
                    hs = slice(h * 128, (h + 1) * 128)
                    P.mm(pN.v(pN.t[:, hs]), Yc.v(Yc.t[:, hs]), Nc.v(Nc.t[:, hs]))
                P.copy(Nx.all(), pN.all())
                if it < NIT - 1:
                    pY2 = ps()
                    for h in range(4):
                        hs = slice(h * 128, (h + 1) * 128)
                        P.mm(pY2.v(pY2.t[:, hs]), Nc.v(Nc.t[:, hs]), Yc.v(Yc.t[:, hs]))
                    P.copy(Yx.all(), pY2.all(), eng="dve")
                yield
                pP = ps()
                for h in range(4):
                    hs = slice(h * 128, (h + 1) * 128)
                    P.mm(pP.v(pP.t[:, hs]), Nx.v(Nx.t[:, hs]), PT.v(PT.t[:, hs]))
                P.tt(PT.all(), PT.all(), pP.all(), ALU.add)
                cur = 1 - cur
                yield

        def back(s):
            PT, qkm, sc1 = PT2[s % 2], QK2[s % 2], SC1[s % 2]
            gb = BG.v(bg3[:, s, 4:8].unsqueeze(2).to_broadcast([128, 4, 128]))
            sbc = lambda a0: sc1.v(sc1.t[:, a0:a0 + 4].unsqueeze(2).to_broadcast([128, 4, 128]))
            pvT = ps()
            pkT = ps()
            for h in range(4):
                hs = slice(h * 128, (h + 1) * 128)
                P.tr(pvT.v(pvT.t[:, hs]), VC[h].v(VC[h].t[:, s * 128:(s + 1) * 128]), cst("ident"))
                P.tr(pkT.v(pkT.t[:, hs]), KN.v(KN.t[:, h, s * 128:(s + 1) * 128], (h,)), cst("ident"))
            yield
            P.tt(Rv.v(h3(Rv.t[:])), pvT.v(h3(pvT.t[:])), BG.v(bg3[:, s, 0:4].unsqueeze(2).to_broadcast([128, 4, 128])),
                 ALU.mult)
            P.tt(Rk.v(h3(Rk.t[:])), pkT.v(h3(pkT.t[:])), sbc(4), ALU.mult)
            P.tt(KD.v(h3(KD.t[:])), pkT.v(h3(pkT.t[:])), sbc(8), ALU.mult)
            yield
            pu = ps()
            pw = ps()
            for h in range(4):
                hs = slice(h * 128, (h + 1) * 128)
                P.mm(pu.v(pu.t[:, hs]), PT.v(PT.t[:, hs]), Rv.v(Rv.t[:, hs]))
                P.mm(pw.v(pw.t[:, hs]), Rk.v(Rk.t[:, hs]), PT.v(PT.t[:, hs]))
            yield
            P.copy(U.all(), pu.all())
            P.copy(WT.all(), pw.all(), eng="dve")
            P.tt(DallB.v(h3(DallB.t[:])), bc_h("ut"), gb, ALU.mult)
            yield
            pe_ = ps()
            P.mm(pe_.all(), cst("ones"), DallB.all())
            P.act(Ebc.all(), pe_.all(), AF.Exp)
            P.tt(QD.v(h3(QD.t[:])), QN.v(QN.t[:, :, s * 128:(s + 1) * 128]), Ebc.v(h3(Ebc.t[:])), ALU.mult)
            yield
            po = PO
            for cl in range(2):
                rows = slice(cl * 64, (cl + 1) * 64)
                pws = ps()
                for h in range(4):
                    hs = slice(h * 128, (h + 1) * 128)
                    P.mm(pws.v(pws.t[:, hs]), WT.v(WT.t[:, hs]), SB.v(SB.t[:, l, h, :], (l,)))
                yield
                P.tt(VN.v(VN.t[rows, :]), U.v(U.t[rows, :]), pws.v(pws.t[rows, :]), ALU.subtract)
                yield
                for h in range(4):
                    c0 = h * 128 + cl * 64
                    P.mm(po.v(po.t[:, c0:c0 + 64]), SB.v(SB.t[:, l, h, :], (l,)), QD.v(QD.t[:, c0:c0 + 64]),
                         start=True, stop=False)
                    P.mm(po.v(po.t[:, c0:c0 + 64]), VN.v(VN.t[rows, h * 128:(h + 1) * 128]),
                         qkm.v(qkm.t[rows, c0:c0 + 64]), start=False, stop=True)
                pSU = ps()
                for h in range(4):
                    hs = slice(h * 128, (h + 1) * 128)
                    P.mm(pSU.v(pSU.t[:, hs]), KD.v(KD.t[rows, hs]), VN.v(VN.t[rows, hs]))
                yield
                ge = Ebc.v(h3(Ebc.t[:])[:, :, cl * 64 + 63:cl * 64 + 64].to_broadcast([128, 4, 128]))
                P.tt(S_v, S_v, ge, ALU.mult)
                P.tt(S_v, S_v, pSU.v(h3(pSU.t[:])), ALU.add)
                yield
            head_epilogue(l, po, gate, s, 4, "gdw")
            yield

        def drive(gens):
            gens = list(gens)
            while gens:
                for g_ in list(gens):
                    try:
                        next(g_)
                    except StopIteration:
                        gens.remove(g_)

        drive([front(0)])
        for s in range(4):
            drive([back(s)] + ([front(s + 1)] if s < 3 else []))
        sfree(Dall, G, GT, nbM, Rv, Rk, U, WT, Ebc, QD, KD, VN, DallB, *PT2, *QK2, *Nn, *Yy, *VC)

    def hid(c):
        if c < 16:
            return MIXT.c(c)
        c -= 16
        if c < 24:
            w = WIDE[c // 8]
            j = (c % 8) // 2
            hf = c % 2
            return w.v(w.t[:, j, :].bitcast(BF16)[:, hf * NT:(hf + 1) * NT], (j,))
        c -= 24
        t = SS[16 + c // 2]
        hf = c % 2
        return t.v(t.t[:].bitcast(BF16)[:, hf * NT:(hf + 1) * NT], (hf,))

    def layer(l, ti):
        if l == 0:
            prenorm(l, "nmp")
        for nm, fn, slabs, m0 in (("A", mixer_A, (0, 1, 2, 3), 0), ("B", mixer_B, (4, 5, 6, 7), 4),
                                  ("C", mixer_C, (8, 9), 8), ("D", mixer_D, (10, 11), 12)):
            if nm == "A" and "A" in mixers and "B" in mixers:
                flags["A_done"] = False
                drive([mixer_A(l), mixer_B(l)])
            elif nm == "B" and "A" in mixers and "B" in mixers:
                pass
            elif nm in mixers:
                r_ = fn(l)
                if r_ is not None:
                    drive([r_])
            else:
                for s_ in slabs:
                    next_slab(l, s_)
                P.memset(MIXT.cs(m0, m0 + 4), 0.0)
        if dbg == ("mix", l) and dbg_d is not None:
            for c in range(16):
                t = salloc()
                P.copy(t.all(), MIXT.c(c))
                P.dma("sp", dbg_d[c * 128:(c + 1) * 128, ti * NT:(ti + 1) * NT], t.all(), "out")
                sfree(t)
        assert len(free) == 28, len(free)
        Y = [SS[i] for i in range(16)]
        free[:] = [i for i in free if i >= 16]
        sqs = salloc()
        for nb in range(4):
            sl = next_slab(l, 12 + nb)
            for j in range(4):
                pz = ps()
                proj_F(sl, j, lambda k: MIXT.c(k), pz.all())
                evac_y(l, "nmo", nb * 4 + j, pz, Y[nb * 4 + j], sqs)
        post_pre(Y, sqs, l, "nfp")
        sfree(sqs)
        for sb in range(16):
            sl = next_slab(l, 16 + sb)
            for j in range(4):
                pz = ps()
                proj_F(sl, j, hT_chunk, pz.all())
                r = CB[j % 2]
                P.act(r.v(r.t[:, 0:NT]), pz.all(), AF.Relu)
                P.tt(hid(sb * 4 + j), r.v(r.t[:, 0:NT]), r.v(r.t[:, 0:NT]), ALU.mult)
        sqs2 = CBQ
        for nb in range(4):
            pz = [ps() for _ in range(4)]
            for g in range(4):
                sl = next_slab(l, 32 + nb * 4 + g)
                for j in range(4):
                    proj_F(sl, j, lambda k: hid(g * 16 + k), pz[j].all(), first=(g == 0), last=(g == 3))
            for j in range(4):
                evac_y(l, "nfo", nb * 4 + j, pz[j], Y[nb * 4 + j], sqs2)
        if l + 1 < depth:
            post_pre(Y, sqs2, l + 1, "nmp")
        else:
            post_pre(Y, sqs2, None, None)
        free[:] = list(range(28))

    for ti in range(ntile):
        P.dma("sp", X.all(), xT_d.rearrange("(c p) t -> p c t", p=128)[:, :, ti * NT:(ti + 1) * NT], "x")
        for l in range(depth):
            layer(l, ti)
        P.dma("sp", yT_d.rearrange("(c p) t -> p c t", p=128)[:, :, ti * NT:(ti + 1) * NT], X.all(), "out")
    P.finalize(final_waits=("out",))
    return nc, P


def kernel(**inp):
    inp = {k: np.asarray(v) for k, v in inp.items()}
    x = inp["x"]
    B, T, _ = x.shape
    depth = inp["w_in"].shape[0]
    nc, _ = build(T, depth)
    wsl, wsm = pack_weights(inp, depth)
    ppk = pack_params(inp, depth)
    cst = make_consts()
    gws = np.ascontiguousarray(inp["gmlp_w_s"].transpose(0, 3, 1, 2).reshape(depth, 128, 512))
    gbs = np.ascontiguousarray(np.broadcast_to(inp["gmlp_b_s"].reshape(depth, 1, 512), (depth, 128, 512)))
    ncore = 8
    active = [0, 1, 4, 5]
    zeros = {"xT": np.zeros((DM, T), np.float32), "wsl": np.zeros_like(wsl), "wsm": np.zeros_like(wsm),
             "pp": np.zeros_like(ppk), "cst": cst, "gws": np.zeros_like(gws), "gbs": np.zeros_like(gbs)}
    in_maps = [zeros] * ncore
    for b in range(B):
        in_maps[active[b]] = {"xT": np.ascontiguousarray(x[b].T), "wsl": wsl, "wsm": wsm, "pp": ppk, "cst": cst,
                              "gws": gws, "gbs": gbs}
    res = run_bass_kernel_spmd(nc, in_maps, core_ids=list(range(ncore)))
    out = np.empty_like(x)
    for b in range(B):
        out[b] = res.results[active[b]]["yT"].T
    return out
```

```python
from contextlib import ExitStack
import numpy as np
import concourse.bass as bass
import concourse.mybir as mybir
from concourse.bass_utils import run_bass_kernel_spmd

F32 = mybir.dt.float32
BF16 = mybir.dt.bfloat16
ALU = mybir.AluOpType
AF = mybir.ActivationFunctionType

import os
NIT = int(os.environ.get('NIT', '5'))
NT = 512
DM = 2048
NSLAB = 48
EPS = 1e-6


class Op:
    __slots__ = ("eng", "fn", "deps", "semkey", "inc", "count", "need")

    def __init__(self, eng, fn, semkey, inc):
        self.eng = eng
        self.fn = fn
        self.deps = []
        self.semkey = semkey
        self.inc = inc
        self.count = None
        self.need = False


class V:
    __slots__ = ("tl", "ap", "cells")

    def __init__(self, tl, ap, cells):
        self.tl = tl
        self.ap = ap
        self.cells = cells


class TL:
    def __init__(self, t, ncell):
        self.t = t
        self.ncell = ncell
        self.w = [None] * ncell
        self.r = [[] for _ in range(ncell)]
        self.psum = False

    def all(self):
        return V(self, self.t[:], range(self.ncell))

    def c(self, i):
        return V(self, self.t[:, i, :], (i,))

    def cs(self, i0, i1):
        return V(self, self.t[:, i0:i1, :], range(i0, i1))

    def v(self, ap, cells=None):
        return V(self, ap, range(self.ncell) if cells is None else cells)


class Prog:
    ENGS = ("pe", "act", "dve", "pool", "sp")

    def __init__(self, nc):
        self.nc = nc
        self.ops = []
        self.es = ExitStack()
        self.n_t = 0

    def sbuf(self, shape, dtype, ncell=1, name=None):
        self.n_t += 1
        t = self.es.enter_context(self.nc.sbuf_tensor(name or f"t{self.n_t}", list(shape), dtype))
        return TL(t, ncell)

    def psum(self, shape, dtype=F32, ncell=1, name=None):
        self.n_t += 1
        t = self.es.enter_context(self.nc.psum_tensor(name or f"p{self.n_t}", list(shape), dtype))
        tl = TL(t, ncell)
        tl.psum = True
        return tl

    def add(self, eng, fn, reads=(), writes=(), semkey=None, inc=1):
        op = Op(eng, fn, semkey or eng, inc)
        deps = {}
        for v in reads:
            if not isinstance(v, V):
                continue
            for c in v.cells:
                w = v.tl.w[c]
                if w is not None:
                    deps[id(w)] = w
                if v.tl.psum:
                    for r in v.tl.r[c]:
                        if r.eng != eng:
                            deps[id(r)] = r
        for v in writes:
            for c in v.cells:
                w = v.tl.w[c]
                if w is not None:
                    deps[id(w)] = w
                for r in v.tl.r[c]:
                    deps[id(r)] = r
        for v in writes:
            for c in v.cells:
                v.tl.w[c] = op
                v.tl.r[c] = []
        for v in reads:
            if not isinstance(v, V):
                continue
            for c in v.cells:
                v.tl.r[c].append(op)
        for d in deps.values():
            if d is op:
                continue
            if d.eng == "pe" and eng == "pe" and d.semkey == "pe":
                continue
            d.need = True
            op.deps.append(d)
        if semkey is not None:
            op.need = True
        self.ops.append(op)
        return op

    @staticmethod
    def _a(x):
        return x.ap if isinstance(x, V) else x

    def mm(self, out, lhsT, rhs, start=True, stop=True):
        a = self._a
        return self.add("pe", lambda e: e.matmul(a(out), a(lhsT), a(rhs), start=start, stop=stop),
                        reads=(lhsT, rhs), writes=(out,))

    def tr(self, out, in_, ident):
        a = self._a
        return self.add("pe", lambda e: e.transpose(a(out), a(in_), a(ident)),
                        reads=(in_, ident), writes=(out,))

    def act(self, out, in_, func, bias=None, scale=None, accum=None):
        a = self._a
        kw = {}
        if bias is not None:
            kw["bias"] = a(bias)
        if scale is not None:
            kw["scale"] = a(scale)
        if accum is not None:
            kw["accum_out"] = a(accum)
        w = (out,) if accum is None else (out, accum)
        return self.add("act", lambda e: e.activation(a(out), a(in_), func, **kw),
                        reads=(in_, bias, scale), writes=w)

    def tt(self, out, in0, in1, op, eng="dve"):
        a = self._a
        return self.add(eng, lambda e: e.tensor_tensor(a(out), a(in0), a(in1), op),
                        reads=(in0, in1), writes=(out,))

    def ts(self, out, in0, s1, s2, op0, op1, eng="dve"):
        a = self._a
        return self.add(eng, lambda e: e.tensor_scalar(a(out), a(in0), a(s1), a(s2), op0, op1),
                        reads=(in0, s1, s2), writes=(out,))

    def stt(self, out, in0, scalar, in1, op0, op1, eng="dve"):
        a = self._a
        return self.add(eng, lambda e: e.scalar_tensor_tensor(a(out), a(in0), a(scalar), a(in1), op0, op1),
                        reads=(in0, scalar, in1), writes=(out,))

    def copy(self, out, in_, eng="act"):
        a = self._a
        if eng == "act":
            return self.add(eng, lambda e: e.copy(a(out), a(in_)), reads=(in_,), writes=(out,))
        return self.add(eng, lambda e: e.tensor_copy(a(out), a(in_)), reads=(in_,), writes=(out,))

    def recip(self, out, in_):
        a = self._a
        return self.add("dve", lambda e: e.reciprocal(a(out), a(in_)), reads=(in_,), writes=(out,))

    def memset(self, out, val, eng="dve"):
        a = self._a
        return self.add(eng, lambda e: e.memset(a(out), val), writes=(out,))

    def scan(self, out, d0, d1, init, op0, op1):
        a = self._a
        return self.add("dve", lambda e: e.tensor_tensor_scan(a(out), a(d0), a(d1), a(init), op0, op1),
                        reads=(d0, d1, init), writes=(out,))

    def dma(self, q, out, in_, group):
        a = self._a
        r = (in_,) if isinstance(in_, V) else ()
        w = (out,) if isinstance(out, V) else ()
        return self.add(q, lambda e: e.dma_start(out=a(out), in_=a(in_)), reads=r, writes=w,
                        semkey="dma_" + group, inc=16)

    def finalize(self, final_waits=()):
        nc = self.nc
        counts = {}
        for op in self.ops:
            if op.need:
                counts[op.semkey] = counts.get(op.semkey, 0) + op.inc
                op.count = counts[op.semkey]
        sems = {}
        for k in counts:
            sems[k] = self.es.enter_context(nc.semaphore("s_" + k))
        byeng = {e: [] for e in self.ENGS}
        for op in self.ops:
            byeng[op.eng].append(op)
        self.stats = {e: len(v) for e, v in byeng.items()}
        self.counts = counts
        block = self.es.enter_context(nc.Block())

        def emit(e, name):
            waited = {}
            for op in byeng[name]:
                for d in op.deps:
                    if waited.get(d.semkey, 0) < d.count:
                        e.wait_ge(sems[d.semkey], d.count)
                        waited[d.semkey] = d.count
                ins = op.fn(e)
                if op.need:
                    ins.then_inc(sems[op.semkey], op.inc)
            if name == "sp":
                for k in final_waits:
                    kk = "dma_" + k
                    if kk in counts:
                        e.wait_ge(sems[kk], counts[kk])

        @block.tensor
        def _(e):
            emit(e, "pe")

        @block.scalar
        def _(e):
            emit(e, "act")

        @block.vector
        def _(e):
            emit(e, "dve")

        @block.gpsimd
        def _(e):
            emit(e, "pool")

        @block.sync
        def _(e):
            emit(e, "sp")

        self.es.close()


PP_FIELDS = [("nmp", 16), ("nmo", 16), ("nfp", 16), ("nfo", 16), ("lb0", 4), ("lb1", 4), ("hgw", 1), ("gdw", 1),
             ("gconv", 48), ("cvw", 124), ("cvb", 4), ("clw", 4), ("clb", 4), ("glw", 4), ("glb", 4),
             ("alog", 4), ("dtb", 4)]
PP_OFF = {}
_o = 0
for _n, _w in PP_FIELDS:
    PP_OFF[_n] = (_o, _w)
    _o += _w
PP_W = _o
CST_NAMES = ["ident", "ut", "sl", "ec", "mu128", "ones"]


def chunked(v):
    return np.ascontiguousarray(v.reshape(-1, 128).T)


def pack_params(inp, depth):
    pp = np.zeros((128, depth, PP_W), np.float32)
    for l in range(depth):
        def put(name, arr):
            o, w = PP_OFF[name]
            pp[:, l, o:o + w] = arr
        put("nmp", chunked(inp["norm_mix_pre"][l]))
        put("nmo", chunked(inp["norm_mix_post"][l]))
        put("nfp", chunked(inp["norm_ff_pre"][l]))
        put("nfo", chunked(inp["norm_ff_post"][l]))
        put("lb0", chunked(inp["lower_bounds"][0]))
        put("lb1", chunked(inp["lower_bounds"][1]))
        put("hgw", inp["hgrn_norm_w"][l].reshape(128, 1))
        put("gdw", inp["gdn_norm_w"][l].reshape(128, 1))
        gc = inp["gdn_conv_w"][l]
        put("gconv", np.concatenate([chunked(gc[k]) for k in range(4)], axis=1))
        cw = inp["conv_dw_w"][l]
        put("cvw", np.concatenate([chunked(cw[k]) for k in range(31)], axis=1))
        put("cvb", chunked(inp["conv_dw_b"][l]))
        put("clw", chunked(inp["conv_ln_w"][l]))
        put("clb", chunked(inp["conv_ln_b"][l]))
        put("glw", chunked(inp["gmlp_ln_w"][l]))
        put("glb", chunked(inp["gmlp_ln_b"][l]))
        put("alog", np.broadcast_to(inp["gdn_a_log"][l][None, :], (128, 4)))
        put("dtb", np.broadcast_to(inp["gdn_dt_bias"][l][None, :], (128, 4)))
    return pp.reshape(128, depth * PP_W)


def make_consts():
    i = np.arange(128)
    same = (i[:, None] // 64) == (i[None, :] // 64)
    c = {
        "ident": np.eye(128),
        "ut": (i[:, None] <= i[None, :]) & same,
        "sl": (i[:, None] > i[None, :]) & same,
        "ec": same,
        "mu128": (i[:, None] <= i[None, :]),
        "ones": np.ones((128, 128)),
    }
    return np.concatenate([c[n].astype(np.float32) for n in CST_NAMES], axis=1)


IN_SLAB_COLS = [0, 512, 1024, 1536, 2048, 2560, 3072, 3584, 4104, 4616, 5128, 5640]


def slabify(w):
    return w.reshape(16, 128, 512).transpose(1, 0, 2).reshape(128, 8192)


def pack_weights(inp, depth):
    wsl = np.empty((depth * NSLAB, 128, 8192), np.float32)
    wsm = np.empty((depth, 128, 128), np.float32)
    for l in range(depth):
        b = l * NSLAB
        w_in = inp["w_in"][l]
        for s, c0 in enumerate(IN_SLAB_COLS):
            wsl[b + s] = slabify(w_in[:, c0:c0 + 512])
        wsm[l] = w_in[:, 4096:4104].reshape(16, 128, 8).transpose(1, 0, 2).reshape(128, 128)
        for s in range(4):
            wsl[b + 12 + s] = slabify(inp["w_out"][l][:, s * 512:(s + 1) * 512])
        for s in range(16):
            wsl[b + 16 + s] = slabify(inp["w_ff1"][l][:, s * 512:(s + 1) * 512])
        w2 = inp["w_ff2"][l]
        for nb in range(4):
            for g in range(4):
                wsl[b + 32 + nb * 4 + g] = slabify(w2[g * 2048:(g + 1) * 2048, nb * 512:(nb + 1) * 512])
    return wsl, wsm


def build(T, depth, dbg=None, mixers="ABCD", bstage=9):
    nc = bass.Bass("TRN2", target_bir_lowering=False)
    ntile = T // NT
    xT_d = nc.dram_tensor("xT", [DM, T], F32, kind="ExternalInput").ap()
    wsl_d = nc.dram_tensor("wsl", [depth * NSLAB, 128, 8192], F32, kind="ExternalInput").ap()
    wsm_d = nc.dram_tensor("wsm", [depth, 128, 128], F32, kind="ExternalInput").ap()
    pp_d = nc.dram_tensor("pp", [128, depth * PP_W], F32, kind="ExternalInput").ap()
    cst_d = nc.dram_tensor("cst", [128, 128 * len(CST_NAMES)], F32, kind="ExternalInput").ap()
    ws_d = nc.dram_tensor("gws", [depth, 128, 512], F32, kind="ExternalInput").ap()
    bs_d = nc.dram_tensor("gbs", [depth, 128, 512], F32, kind="ExternalInput").ap()
    yT_d = nc.dram_tensor("yT", [DM, T], F32, kind="ExternalOutput").ap()
    dbg_d = None
    if dbg:
        dbg_d = nc.dram_tensor("dbg", [DM, T], F32, kind="ExternalOutput").ap()

    P = Prog(nc)
    X = P.sbuf([128, 16, NT], F32, 16, "X")
    HT = P.sbuf([128, 16, NT], BF16, 16, "HT")
    MIXT = P.sbuf([128, 16, NT], BF16, 16, "MIXT")
    WIDE = [P.sbuf([128, 4, NT], F32, 4, f"WIDE{i}") for i in range(3)]
    SS = [P.sbuf([128, NT], F32, 2, f"S{i}") for i in range(28)]
    SLAB = [P.sbuf([128, 16, 512], BF16, 1, f"SLAB{i}") for i in range(2)]
    CB = [P.sbuf([128, 544], F32, 1, f"CB{i}") for i in range(2)]
    CBQ = TL(CB[0].t, 2)
    PPT = P.sbuf([128, depth * PP_W], F32, 1, "PPT")
    CST = P.sbuf([128, 128 * len(CST_NAMES)], F32, 1, "CST")
    WSM = P.sbuf([128, depth, 16, 8], BF16, 1, "WSM")
    ONEB = P.sbuf([128, 128], BF16, 1, "ONEB")
    ONEF = P.sbuf([128, NT], F32, 1, "ONEF")
    GWS = P.sbuf([128, depth, 4, 128], F32, 1, "GWS")
    R2 = P.sbuf([128, depth, 4, 128], F32, 1, "R2")
    LB = P.sbuf([128, depth, 4], F32, 1, "LB")
    OML = P.sbuf([128, depth, 4], F32, 1, "OML")
    NOML = P.sbuf([128, depth, 4], F32, 1, "NOML")
    NEGA = P.sbuf([128, depth, 4], F32, 1, "NEGA")
    SA = P.sbuf([128, depth, 4, 128], F32, depth, "SA")
    SB = P.sbuf([128, depth, 4, 128], F32, depth, "SB")
    HD = P.sbuf([128, depth, 4, 30], F32, depth, "HD")
    HG = P.sbuf([128, depth, 12, 3], F32, depth, "HG")
    SM = [P.sbuf([128, 32], F32, 1, f"SM{i}") for i in range(16)]
    PS = [P.psum([128, 512], F32, 1, f"PS{i}") for i in range(8)]
    st = {"ps": 0, "sm": 0}

    PN = PS[0]
    PO = PS[1]

    def ps():
        st["ps"] = (st["ps"] + 1) % 6
        return PS[2 + st["ps"]]

    def sm():
        st["sm"] = (st["sm"] + 1) % 14
        return SM[st["sm"]]

    def cst(name):
        i = CST_NAMES.index(name)
        return CST.v(CST.t[:, i * 128:(i + 1) * 128])

    def pp(l, name, c=None):
        o, w = PP_OFF[name]
        o += l * PP_W
        if c is None:
            return PPT.v(PPT.t[:, o:o + w])
        return PPT.v(PPT.t[:, o + c:o + c + 1])

    free = list(range(28))

    def salloc():
        return SS[free.pop(0)]

    def sfree(*ts):
        for t in ts:
            free.append(SS.index(t))

    def h3(v_ap):
        return v_ap.rearrange("p (h i) -> p h i", h=4)

    P.dma("sp", PPT.all(), pp_d, "c0")
    P.dma("sp", CST.all(), cst_d, "c1")
    P.dma("pool", WSM.v(WSM.t[:].rearrange("p l k n -> p l (k n)")), wsm_d.rearrange("l p n -> p l n"), "c2")
    P.memset(ONEB.all(), 1.0)
    P.memset(ONEF.all(), 1.0)
    P.memset(SA.all(), 0.0)
    P.memset(SB.all(), 0.0)
    P.memset(HD.all(), 0.0)
    P.memset(HG.all(), 0.0)
    P.memset(LB.all(), 0.0)
    for l in range(1, depth):
        t = sm()
        P.tt(t.v(t.t[:, 0:4]), pp(0, "lb1"), pp(0, "lb0"), ALU.subtract)
        P.act(LB.v(LB.t[:, l, :]), t.v(t.t[:, 0:4]), AF.Sigmoid)
    P.ts(OML.all(), LB.all(), -1.0, 1.0, ALU.mult, ALU.add)
    P.ts(NOML.all(), LB.all(), 1.0, -1.0, ALU.mult, ALU.add)
    for l in range(depth):
        P.act(NEGA.v(NEGA.t[:, l, :]), pp(l, "alog"), AF.Exp)
    P.ts(NEGA.all(), NEGA.all(), -1.0, 0.0, ALU.mult, ALU.add)
    for l in range(depth):
        tw = salloc()
        tb = salloc()
        P.dma("sp", tw.all(), ws_d[l], f"c3_{l}")
        P.dma("sp", tb.all(), bs_d[l], f"c4_{l}")
        P.tt(GWS.v(GWS.t[:, l, :, :]), tw.v(h3(tw.t[:])), CST.v(cst("mu128").ap.unsqueeze(1).to_broadcast([128, 4, 128])),
             ALU.mult)
        pr = ps()
        P.mm(pr.all(), cst("ones"), GWS.v(GWS.t[:, l, :, :].rearrange("p h i -> p (h i)")))
        for h in range(4):
            P.stt(R2.v(R2.t[:, l, h, :]), pr.v(pr.t[:, h * 128:(h + 1) * 128]), pp(l, "glb", h),
                  tb.v(tb.t[:, h * 128:(h + 1) * 128]), ALU.mult, ALU.add)
        sfree(tw, tb)

    slab_seq = [(ti, l, s) for ti in range(ntile) for l in range(depth) for s in range(NSLAB)]
    wst = {"issued": 0, "used": 0}

    def issue_slab():
        i = wst["issued"]
        if i >= len(slab_seq):
            return
        _, l, s = slab_seq[i]
        buf = SLAB[i % 2]
        P.dma("pool", buf.v(buf.t[:].rearrange("p k n -> p (k n)")), wsl_d[l * NSLAB + s], f"w{i % 2}")
        wst["issued"] += 1

    def next_slab(l, s):
        i = wst["used"]
        assert slab_seq[i][1] == l and slab_seq[i][2] == s, (slab_seq[i], l, s)
        while wst["issued"] < min(i + 2, len(slab_seq)):
            issue_slab()
        wst["used"] += 1
        return SLAB[i % 2]

    def proj_F(slab, j, rhs_chunks, out_v, first=True, last=True):
        for k in range(16):
            P.mm(out_v, slab.v(slab.t[:, k, j * 128:(j + 1) * 128]), rhs_chunks(k),
                 start=(first and k == 0), stop=(last and k == 15))

    def proj_T(slab, sub, out_v):
        for k in range(16):
            P.mm(out_v, HT.v(HT.t[:, k, sub * 128:(sub + 1) * 128], (k,)), slab.v(slab.t[:, k, :]),
                 start=(k == 0), stop=(k == 15))

    def rstd_from_ms(ms_v, out_v, scale):
        P.act(out_v, ms_v, AF.Sqrt, bias=EPS, scale=scale)
        P.recip(out_v, out_v)

    def prenorm(l, wname):
        sq = salloc()
        rs = salloc()
        pm = PN
        for c in range(16):
            P.act(sq.v(sq.t[:].bitcast(BF16)[:, 0:NT], (0,)), X.c(c), AF.Square)
            P.mm(pm.all(), ONEB.all(), sq.v(sq.t[:].bitcast(BF16)[:, 0:NT], (0,)), start=(c == 0), stop=(c == 15))
        rstd_from_ms(pm.all(), rs.all(), 1.0 / DM)
        for c in range(16):
            P.stt(HT.c(c), X.c(c), pp(l, wname, c), rs.all(), ALU.mult, ALU.mult)
        sfree(sq, rs)

    def postnorm_add(l, wname, Y):
        sq = salloc()
        rs = salloc()
        pm = ps()
        for c in range(16):
            P.act(sq.v(sq.t[:].bitcast(BF16)[:, 0:NT], (0,)), Y[c].all(), AF.Square)
            P.mm(pm.all(), ONEB.all(), sq.v(sq.t[:].bitcast(BF16)[:, 0:NT], (0,)), start=(c == 0), stop=(c == 15))
        rstd_from_ms(pm.all(), rs.all(), 1.0 / DM)
        for c in range(16):
            P.stt(Y[c].all(), Y[c].all(), pp(l, wname, c), rs.all(), ALU.mult, ALU.mult)
            P.tt(X.c(c), X.c(c), Y[c].all(), ALU.add)
        sfree(sq, rs)

    nst = {"i": 0}

    def evac_y(l, wname, c, pz, Yc, sqs):
        hf = nst["i"] % 2
        nst["i"] += 1
        sqv = sqs.v(sqs.t[:].bitcast(BF16)[:, hf * NT:(hf + 1) * NT], (hf,))
        P.act(sqv, pz.all(), AF.Square)
        P.mm(PN.all(), ONEB.all(), sqv, start=(c == 0), stop=(c == 15))
        P.act(Yc.all(), pz.all(), AF.Copy, scale=pp(l, wname, c))

    def post_pre(Y, sqs, lpre, wpre):
        rs = salloc()
        rstd_from_ms(PN.all(), rs.all(), 1.0 / DM)
        for c in range(16):
            P.tt(Y[c].all(), Y[c].all(), rs.all(), ALU.mult, eng="pool")
            P.tt(X.c(c), X.c(c), Y[c].all(), ALU.add)
            if wpre is not None:
                hf = c % 2
                sqv = sqs.v(sqs.t[:].bitcast(BF16)[:, hf * NT:(hf + 1) * NT], (hf,))
                P.act(sqv, X.c(c), AF.Square)
                P.mm(PN.all(), ONEB.all(), sqv, start=(c == 0), stop=(c == 15))
        if wpre is not None:
            rstd_from_ms(PN.all(), rs.all(), 1.0 / DM)
            for c in range(16):
                P.stt(HT.c(c), X.c(c), pp(lpre, wpre, c), rs.all(), ALU.mult, ALU.mult)
        sfree(rs)

    def head_epilogue(l, po, gate, s, mix0, wname):
        sq = salloc()
        rs = salloc()
        P.act(sq.all(), po.all(), AF.Square)
        pm = ps()
        P.mm(pm.all(), cst("ones"), sq.all())
        rstd_from_ms(pm.all(), rs.all(), 1.0 / 128)
        P.tt(sq.all(), po.all(), rs.all(), ALU.mult)
        P.stt(MIXT.v(MIXT.t[:, mix0:mix0 + 4, s * 128:(s + 1) * 128], range(mix0, mix0 + 4)), sq.v(h3(sq.t[:])),
              pp(l, wname), gate.v(gate.t[:, :, s * 128:(s + 1) * 128]), ALU.mult, ALU.mult)
        sfree(sq, rs)

    hT_chunk = lambda k: HT.c(k)

    def mixer_D(l):
        sa = next_slab(l, 10)
        pa = [ps() for _ in range(4)]
        for c in range(4):
            proj_F(sa, c, hT_chunk, pa[c].all())
        ya = [salloc() for _ in range(4)]
        for c in range(4):
            P.copy(ya[c].all(), pa[c].all())
        sg = next_slab(l, 11)
        acc = [salloc() for _ in range(4)]
        for c in range(4):
            pg = ps()
            proj_F(sg, c, hT_chunk, pg.all())
            P.act(acc[c].all(), pg.all(), AF.Sigmoid)
        for c0 in (0, 2):
            pair = (c0, c0 + 1)
            for c in pair:
                cb = CB[c % 2]
                P.copy(cb.v(cb.t[:, 0:30]), HD.v(HD.t[:, l, c, :], (l,)), eng="dve")
                P.tt(cb.v(cb.t[:, 30:542]), ya[c].all(), acc[c].all(), ALU.mult)
                P.copy(HD.v(HD.t[:, l, c, :], (l,)), cb.v(cb.t[:, 512:542]), eng="dve")
            for c in pair:
                cb = CB[c % 2]
                P.ts(acc[c].all(), cb.v(cb.t[:, 0:512]), pp(l, "cvw", 0 * 4 + c), pp(l, "cvb", c), ALU.mult, ALU.add)
            for k in range(1, 31):
                for c in pair:
                    cb = CB[c % 2]
                    P.stt(acc[c].all(), cb.v(cb.t[:, k:k + 512]), pp(l, "cvw", k * 4 + c), acc[c].all(), ALU.mult,
                          ALU.add)
        pmean = ps()
        pex2 = ps()
        for c in range(4):
            P.mm(pmean.all(), cst("ones"), acc[c].all(), start=(c == 0), stop=(c == 3))
        for c in range(4):
            P.act(ya[c].all(), acc[c].all(), AF.Square)
            P.mm(pex2.all(), cst("ones"), ya[c].all(), start=(c == 0), stop=(c == 3))
        mean = ya[0]
        rs = ya[1]
        var = ya[2]
        P.act(mean.all(), pmean.all(), AF.Copy, scale=1.0 / 512)
        P.tt(var.all(), mean.all(), mean.all(), ALU.mult)
        P.stt(var.all(), pex2.all(), 1.0 / 512, var.all(), ALU.mult, ALU.subtract)
        rstd_from_ms(var.all(), rs.all(), 1.0)
        for c in range(4):
            P.tt(acc[c].all(), acc[c].all(), mean.all(), ALU.subtract)
            P.tt(acc[c].all(), acc[c].all(), rs.all(), ALU.mult)
            P.act(MIXT.c(12 + c), acc[c].all(), AF.Silu, bias=pp(l, "clb", c), scale=pp(l, "clw", c))
        sfree(*ya, *acc)

    def mixer_C(l):
        su = next_slab(l, 8)
        gu = [salloc() for _ in range(4)]
        for c in range(4):
            pu = ps()
            proj_F(su, c, hT_chunk, pu.all())
            P.act(gu[c].all(), pu.all(), AF.Gelu)
        sv = next_slab(l, 9)
        vn = [salloc() for _ in range(4)]
        for s in range(4):
            pv = ps()
            proj_T(sv, s, pv.all())
            t1, t2, t3 = sm(), sm(), sm()
            sq = salloc()
            P.memset(t1.v(t1.t[:, 0:1]), 0.0)
            P.memset(t2.v(t2.t[:, 0:1]), 0.0)
            P.act(vn[s].all(), pv.all(), AF.Gelu, accum=t1.v(t1.t[:, 0:1]))
            P.act(sq.all(), vn[s].all(), AF.Square, accum=t2.v(t2.t[:, 0:1]))
            sfree(sq)
            P.ts(t1.v(t1.t[:, 1:2]), t1.v(t1.t[:, 0:1]), 1.0 / 512, 0.0, ALU.mult, ALU.add)
            P.tt(t3.v(t3.t[:, 0:1]), t1.v(t1.t[:, 1:2]), t1.v(t1.t[:, 1:2]), ALU.mult)
            P.stt(t3.v(t3.t[:, 1:2]), t2.v(t2.t[:, 0:1]), 1.0 / 512, t3.v(t3.t[:, 0:1]), ALU.mult, ALU.subtract)
            rstd_from_ms(t3.v(t3.t[:, 1:2]), t3.v(t3.t[:, 2:3]), 1.0)
            P.ts(vn[s].all(), vn[s].all(), t1.v(t1.t[:, 1:2]), t3.v(t3.t[:, 2:3]), ALU.subtract, ALU.mult)
        for h in range(4):
            pmx = ps()
            for s in range(4):
                P.mm(pmx.v(pmx.t[:, s * 128:(s + 1) * 128]), vn[s].v(vn[s].t[:, h * 128:(h + 1) * 128]),
                     GWS.v(GWS.t[:, l, h, :]))
            tmp = salloc()
            P.stt(tmp.v(h3(tmp.t[:])), pmx.v(h3(pmx.t[:])), pp(l, "glw", h),
                  R2.v(R2.t[:, l, h, :].unsqueeze(1).to_broadcast([128, 4, 128])), ALU.mult, ALU.add)
            P.tt(MIXT.c(8 + h), tmp.all(), gu[h].all(), ALU.mult)
            sfree(tmp)
        sfree(*gu, *vn)

    def mixer_A(l):
        gate = WIDE[0]
        sq_ = next_slab(l, 0)
        qe = [salloc() for _ in range(4)]
        ke = [salloc() for _ in range(4)]
        keT = [salloc() for _ in range(4)]
        SC = salloc()
        scv = SC.t[:].rearrange("p (h k c) -> p h k c", h=4, k=16)
        for h in range(4):
            pq = ps()
            proj_F(sq_, h, hT_chunk, pq.all())
            P.act(qe[h].all(), pq.all(), AF.Silu)
        sf = next_slab(l, 1)
        sigs = []
        for h in range(4):
            pf = ps()
            proj_F(sf, h, hT_chunk, pf.all())
            sig = salloc()
            P.act(sig.all(), pf.all(), AF.Sigmoid)
            sigs.append(sig)
        sv = next_slab(l, 2)
        vT = [salloc() for _ in range(4)]
        for s in range(4):
            pv = ps()
            proj_T(sv, s, pv.all())
            P.copy(vT[s].all(), pv.all())
        sg = next_slab(l, 3)
        for h in range(4):
            pg = ps()
            proj_F(sg, h, hT_chunk, pg.all())
            P.act(gate.c(h), pg.all(), AF.Silu)
        for h in range(4):
            sig = sigs[h]
            lf, B = salloc(), salloc()
            P.ts(lf.all(), sig.all(), OML.v(OML.t[:, l, h:h + 1]), LB.v(LB.t[:, l, h:h + 1]), ALU.mult, ALU.add)
            P.act(lf.all(), lf.all(), AF.Ln)
            P.ts(ke[h].all(), sig.all(), NOML.v(NOML.t[:, l, h:h + 1]), OML.v(OML.t[:, l, h:h + 1]), ALU.mult,
                 ALU.add)
            P.scan(B.all(), ONEF.all(), lf.all(), 0.0, ALU.mult, ALU.add)
            B3 = B.t[:].rearrange("p (c t) -> p c t", t=64)
            bs_ = sm()
            P.memset(bs_.v(bs_.t[:, 0:1]), 0.0)
            P.copy(bs_.v(bs_.t[:, 1:8]), B.v(B3[:, 0:7, 63]), eng="dve")
            bm, be = sm(), sm()
            P.tt(bm.v(bm.t[:, 0:8]), B.v(B3[:, :, 31]), bs_.v(bs_.t[:, 0:8]), ALU.subtract)
            P.tt(be.v(be.t[:, 0:8]), B.v(B3[:, :, 63]), bs_.v(bs_.t[:, 0:8]), ALU.subtract)
            P.act(SC.v(scv[:, h, 0, :]), bm.v(bm.t[:, 0:8]), AF.Exp)
            P.act(SC.v(scv[:, h, 1, :]), be.v(be.t[:, 0:8]), AF.Exp)
            P.tt(be.v(be.t[:, 0:8]), be.v(be.t[:, 0:8]), bm.v(bm.t[:, 0:8]), ALU.subtract)
            P.act(SC.v(scv[:, h, 2, :]), be.v(be.t[:, 0:8]), AF.Exp)
            bp = lf
            P.tt(bp.v(bp.t[:].rearrange("p (c t) -> p c t", t=64)), B.v(B3),
                 B.v(B3[:, :, 31:32].to_broadcast([128, 8, 64])), ALU.subtract)
            P.act(sig.all(), bp.all(), AF.Exp)
            P.tt(qe[h].all(), qe[h].all(), sig.all(), ALU.mult)
            P.act(sig.all(), bp.all(), AF.Exp, scale=-1.0)
            P.tt(ke[h].all(), ke[h].all(), sig.all(), ALU.mult)
            sfree(sig, lf, B)
        for h in range(4):
            pt = ps()
            for s in range(4):
                P.tr(pt.v(pt.t[:, s * 128:(s + 1) * 128]), ke[h].v(ke[h].t[:, s * 128:(s + 1) * 128]), cst("ident"))
            P.copy(keT[h].all(), pt.all())
        Sp = salloc()
        tmp = salloc()
        smk = salloc()
        S_v = SA.v(SA.t[:, l, :, :], (l,))

        def scb(k, cc):
            return SC.v(scv[:, :, k, cc:cc + 1].to_broadcast([128, 4, 128]))

        for s in range(4):
            pS = ps()
            for h in range(4):
                P.mm(pS.v(pS.t[:, h * 128:(h + 1) * 128]), ke[h].v(ke[h].t[:, s * 128:(s + 1) * 128]),
                     qe[h].v(qe[h].t[:, s * 128:(s + 1) * 128]))
            P.tt(smk.v(h3(smk.t[:])), pS.v(h3(pS.t[:])), CST.v(cst("ut").ap.unsqueeze(1).to_broadcast([128, 4, 128])),
                 ALU.mult)
            po = PO
            for cl in range(2):
                cc = s * 2 + cl
                P.tt(Sp.v(h3(Sp.t[:])), S_v, scb(0, cc), ALU.mult)
                for h in range(4):
                    c0 = h * 128 + cl * 64
                    P.mm(po.v(po.t[:, c0:c0 + 64]), vT[s].v(vT[s].t[:, h * 128:(h + 1) * 128]),
                         smk.v(smk.t[:, c0:c0 + 64]), start=True, stop=False)
                    P.mm(po.v(po.t[:, c0:c0 + 64]), Sp.v(Sp.t[:, h * 128:(h + 1) * 128]),
                         qe[h].v(qe[h].t[:, cc * 64:(cc + 1) * 64]), start=False, stop=True)
                pU = ps()
                for h in range(4):
                    P.mm(pU.v(pU.t[:, h * 128:(h + 1) * 128]),
                         keT[h].v(keT[h].t[cl * 64:(cl + 1) * 64, s * 128:(s + 1) * 128]),
                         vT[s].v(vT[s].t[cl * 64:(cl + 1) * 64, h * 128:(h + 1) * 128]))
                P.tt(S_v, S_v, scb(1, cc), ALU.mult)
                P.tt(tmp.v(h3(tmp.t[:])), pU.v(h3(pU.t[:])), scb(2, cc), ALU.mult)
                P.tt(S_v, S_v, tmp.v(h3(tmp.t[:])), ALU.add)
            head_epilogue(l, po, gate, s, 0, "hgw")
        sfree(*qe, *ke, *keT, *vT, SC, Sp, tmp, smk)

    def mixer_B(l):
        QN, KN, gate = WIDE[1], WIDE[2], WIDE[0]
        VC = [salloc() for _ in range(4)]
        dsts = [lambda h: QN.c(h), lambda h: KN.c(h), lambda h: VC[h].all()]
        for which in range(3):
            sl = next_slab(l, 4 + which)
            for h in range(4):
                ch = which * 4 + h
                pq = ps()
                proj_F(sl, h, hT_chunk, pq.all())
                cb = CB[h % 2]
                P.copy(cb.v(cb.t[:, 0:3]), HG.v(HG.t[:, l, ch, :], (l,)), eng="dve")
                P.copy(cb.v(cb.t[:, 3:515]), pq.all())
                P.copy(HG.v(HG.t[:, l, ch, :], (l,)), cb.v(cb.t[:, 512:515]), eng="dve")
                acc = salloc()
                P.ts(acc.all(), cb.v(cb.t[:, 0:512]), pp(l, "gconv", 0 * 12 + ch), 0.0, ALU.mult, ALU.add)
                for k in range(1, 4):
                    P.stt(acc.all(), cb.v(cb.t[:, k:k + 512]), pp(l, "gconv", k * 12 + ch), acc.all(), ALU.mult,
                          ALU.add)
                P.act(dsts[which](h), acc.all(), AF.Silu)
                sfree(acc)
        for which, W in enumerate((QN, KN)):
            for h in range(4):
                sq, rs = salloc(), salloc()
                P.act(sq.all(), W.c(h), AF.Square)
                pm = ps()
                P.mm(pm.all(), cst("ones"), sq.all())
                rstd_from_ms(pm.all(), rs.all(), 1.0)
                if which == 0:
                    P.stt(W.c(h), W.c(h), 128 ** -0.5, rs.all(), ALU.mult, ALU.mult)
                else:
                    P.tt(W.c(h), W.c(h), rs.all(), ALU.mult)
                sfree(sq, rs)
        sz = next_slab(l, 7)
        for h in range(4):
            pg = ps()
            proj_F(sz, h, hT_chunk, pg.all())
            P.act(gate.c(h), pg.all(), AF.Silu)
        pbg = ps()
        for s in range(4):
            for k in range(16):
                P.mm(pbg.v(pbg.t[:, s * 8:(s + 1) * 8]), HT.v(HT.t[:, k, s * 128:(s + 1) * 128], (k,)),
                     WSM.v(WSM.t[:, l, k, :]), start=(k == 0), stop=(k == 15))
        BG = sm()
        bg3 = BG.t[:, 0:32].rearrange("p (s n) -> p s n", n=8)
        pbg3 = pbg.t[:, 0:32].rearrange("p (s n) -> p s n", n=8)
        tg = sm()
        tg3 = tg.t[:, 0:16].rearrange("p (s n) -> p s n", n=4)
        P.act(BG.v(bg3[:, :, 0:4]), pbg.v(pbg3[:, :, 0:4]), AF.Sigmoid)
        P.tt(tg.v(tg3), pbg.v(pbg3[:, :, 4:8]), PPT.v(pp(l, "dtb").ap.unsqueeze(1).to_broadcast([128, 4, 4])), ALU.add)
        P.act(tg.v(tg3), tg.v(tg3), AF.Exp)
        P.act(tg.v(tg3), tg.v(tg3), AF.Ln, bias=1.0)
        P.tt(BG.v(bg3[:, :, 4:8]), tg.v(tg3), NEGA.v(NEGA.t[:, l, :].unsqueeze(1).to_broadcast([128, 4, 4])), ALU.mult)

        if bstage <= 1:
            P.memset(MIXT.cs(4, 8), 0.0)
            sfree(*VC)
            return
        Dall, G, GT, nbM, Rv, Rk, U, WT, Ebc, QD, KD, VN, DallB = [salloc() for _ in range(13)]
        PT2 = [salloc(), salloc()]
        QK2 = [salloc(), salloc()]
        Nn = [salloc(), salloc()]
        Yy = [salloc(), salloc()]
        SC1 = [SM[14], SM[15]]
        S_v = SB.v(SB.t[:, l, :, :], (l,))
        bc_h = lambda name: CST.v(cst(name).ap.unsqueeze(1).to_broadcast([128, 4, 128]))

        def front(s):
            PT, qkm, sc1 = PT2[s % 2], QK2[s % 2], SC1[s % 2]
            beta = BG.v(bg3[:, s, 0:4])
            g = BG.v(bg3[:, s, 4:8])
            gb = BG.v(bg3[:, s, 4:8].unsqueeze(2).to_broadcast([128, 4, 128]))
            pg2 = ps()
            P.mm(pg2.v(pg2.t[:, 0:4]), cst("ut"), g)
            P.mm(pg2.v(pg2.t[:, 4:8]), cst("ec"), g)
            gcs = sm()
            P.copy(gcs.v(gcs.t[:, 0:8]), pg2.v(pg2.t[:, 0:8]))
            yield
            P.act(sc1.v(sc1.t[:, 0:4]), gcs.v(gcs.t[:, 0:4]), AF.Exp)
            P.tt(sc1.v(sc1.t[:, 4:8]), sc1.v(sc1.t[:, 0:4]), beta, ALU.mult)
            P.tt(sc1.v(sc1.t[:, 8:12]), gcs.v(gcs.t[:, 4:8]), gcs.v(gcs.t[:, 0:4]), ALU.subtract)
            P.act(sc1.v(sc1.t[:, 8:12]), sc1.v(sc1.t[:, 8:12]), AF.Exp)
            P.ts(sc1.v(sc1.t[:, 12:16]), beta, -1.0, 0.0, ALU.mult, ALU.add)
            sbc = lambda a0: sc1.v(sc1.t[:, a0:a0 + 4].unsqueeze(2).to_broadcast([128, 4, 128]))
            P.tt(Dall.v(h3(Dall.t[:])), bc_h("sl"), gb, ALU.mult)
            yield
            pr = ps()
            P.mm(pr.all(), cst("ut"), Dall.all())
            prT = ps()
            for h in range(4):
                P.mm(prT.v(prT.t[:, h * 128:(h + 1) * 128]), Dall.v(Dall.t[:, h * 128:(h + 1) * 128]), cst("ut"))
            yield
            P.act(G.all(), pr.all(), AF.Exp)
            P.act(GT.all(), prT.all(), AF.Exp)
            P.tt(GT.v(h3(GT.t[:])), GT.v(h3(GT.t[:])), bc_h("ut"), ALU.mult)
            P.tt(nbM.v(h3(nbM.t[:])), bc_h("sl"), sbc(12), ALU.mult)
            yield
            pA = ps()
            pQ = ps()
            for h in range(4):
                kcols = KN.v(KN.t[:, h, s * 128:(s + 1) * 128], (h,))
                P.mm(pA.v(pA.t[:, h * 128:(h + 1) * 128]), kcols, kcols)
                P.mm(pQ.v(pQ.t[:, h * 128:(h + 1) * 128]), kcols, QN.v(QN.t[:, h, s * 128:(s + 1) * 128], (h,)))
            yield
            N0, Y0 = Nn[0], Yy[0]
            P.tt(N0.all(), pA.all(), nbM.all(), ALU.mult)
            P.tt(N0.all(), N0.all(), G.all(), ALU.mult)
            P.tt(qkm.all(), pQ.all(), GT.all(), ALU.mult)
            yield
            pY = ps()
            for h in range(4):
                P.mm(pY.v(pY.t[:, h * 128:(h + 1) * 128]), N0.v(N0.t[:, h * 128:(h + 1) * 128]), cst("ident"))
            P.copy(Y0.all(), pY.all())
            P.tt(PT.v(h3(PT.t[:])), Y0.v(h3(Y0.t[:])), bc_h("ident"), ALU.add)
            yield
            cur = 0
            for it in range(NIT):
                Nc, Yc, Nx, Yx = Nn[cur], Yy[cur], Nn[1 - cur], Yy[1 - cur]
                pN = ps()
                for h in range(4):
                    hs = slice(h * 128, (h + 1) * 128)
                    P.mm(pN.v(pN.t[:, hs]), Yc.v(Yc.t[:, hs]), Nc.v(Nc.t[:, hs]))
                P.copy(Nx.all(), pN.all())
                if it < NIT - 1:
                    pY2 = ps()
                    for h in range(4):
                        hs = slice(h * 128, (h + 1) * 128)
                        P.mm(pY2.v(pY2.t[:, hs]), Nc.v(Nc.t[:, hs]), Yc.v(Yc.t[:, hs]))
                    P.copy(Yx.all(), pY2.all(), eng="dve")
                yield
                pP = ps()
                for h in range(4):
                    hs = slice(h * 128, (h + 1) * 128)
                    P.mm(pP.v(pP.t[:, hs]), Nx.v(Nx.t[:, hs]), PT.v(PT.t[:, hs]))
                P.tt(PT.all(), PT.all(), pP.all(), ALU.add)
                cur = 1 - cur
                yield

        def back(s):
            PT, qkm, sc1 = PT2[s % 2], QK2[s % 2], SC1[s % 2]
            gb = BG.v(bg3[:, s, 4:8].unsqueeze(2).to_broadcast([128, 4, 128]))
            sbc = lambda a0: sc1.v(sc1.t[:, a0:a0 + 4].unsqueeze(2).to_broadcast([128, 4, 128]))
            pvT = ps()
            pkT = ps()
            for h in range(4):
                hs = slice(h * 128, (h + 1) * 128)
                P.tr(pvT.v(pvT.t[:, hs]), VC[h].v(VC[h].t[:, s * 128:(s + 1) * 128]), cst("ident"))
                P.tr(pkT.v(pkT.t[:, hs]), KN.v(KN.t[:, h, s * 128:(s + 1) * 128], (h,)), cst("ident"))
            yield
            P.tt(Rv.v(h3(Rv.t[:])), pvT.v(h3(pvT.t[:])), BG.v(bg3[:, s, 0:4].unsqueeze(2).to_broadcast([128, 4, 128])),
                 ALU.mult)
            P.tt(Rk.v(h3(Rk.t[:])), pkT.v(h3(pkT.t[:])), sbc(4), ALU.mult)
            P.tt(KD.v(h3(KD.t[:])), pkT.v(h3(pkT.t[:])), sbc(8), ALU.mult)
            yield
            pu = ps()
            pw = ps()
            for h in range(4):
                hs = slice(h * 128, (h + 1) * 128)
                P.mm(pu.v(pu.t[:, hs]), PT.v(PT.t[:, hs]), Rv.v(Rv.t[:, hs]))
                P.mm(pw.v(pw.t[:, hs]), Rk.v(Rk.t[:, hs]), PT.v(PT.t[:, hs]))
            yield
            P.copy(U.all(), pu.all())
            P.copy(WT.all(), pw.all(), eng="dve")
            P.tt(DallB.v(h3(DallB.t[:])), bc_h("ut"), gb, ALU.mult)
            yield
            pe_ = ps()
            P.mm(pe_.all(), cst("ones"), DallB.all())
            P.act(Ebc.all(), pe_.all(), AF.Exp)
            P.tt(QD.v(h3(QD.t[:])), QN.v(QN.t[:, :, s * 128:(s + 1) * 128]), Ebc.v(h3(Ebc.t[:])), ALU.mult)
            yield
            po = PO
            for cl in range(2):
                rows = slice(cl * 64, (cl + 1) * 64)
                pws = ps()
                for h in range(4):
                    hs = slice(h * 128, (h + 1) * 128)
                    P.mm(pws.v(pws.t[:, hs]), WT.v(WT.t[:, hs]), SB.v(SB.t[:, l, h, :], (l,)))
                yield
                P.tt(VN.v(VN.t[rows, :]), U.v(U.t[rows, :]), pws.v(pws.t[rows, :]), ALU.subtract)
                yield
                for h in range(4):
                    c0 = h * 128 + cl * 64
                    P.mm(po.v(po.t[:, c0:c0 + 64]), SB.v(SB.t[:, l, h, :], (l,)), QD.v(QD.t[:, c0:c0 + 64]),
                         start=True, stop=False)
                    P.mm(po.v(po.t[:, c0:c0 + 64]), VN.v(VN.t[rows, h * 128:(h + 1) * 128]),
                         qkm.v(qkm.t[rows, c0:c0 + 64]), start=False, stop=True)
                pSU = ps()
                for h in range(4):
                    hs = slice(h * 128, (h + 1) * 128)
                    P.mm(pSU.v(pSU.t[:, hs]), KD.v(KD.t[rows, hs]), VN.v(VN.t[rows, hs]))
                yield
                ge = Ebc.v(h3(Ebc.t[:])[:, :, cl * 64 + 63:cl * 64 + 64].to_broadcast([128, 4, 128]))
                P.tt(S_v, S_v, ge, ALU.mult)
                P.tt(S_v, S_v, pSU.v(h3(pSU.t[:])), ALU.add)
                yield
            head_epilogue(l, po, gate, s, 4, "gdw")
            yield

        def drive(gens):
            gens = list(gens)
            while gens:
                for g_ in list(gens):
                    try:
                        next(g_)
                    except StopIteration:
                        gens.remove(g_)

        drive([front(0)])
        for s in range(4):
            drive([back(s)] + ([front(s + 1)] if s < 3 else []))
        sfree(Dall, G, GT, nbM, Rv, Rk, U, WT, Ebc, QD, KD, VN, DallB, *PT2, *QK2, *Nn, *Yy, *VC)

    def hid(c):
        if c < 16:
            return MIXT.c(c)
        c -= 16
        if c < 24:
            w = WIDE[c // 8]
            j = (c % 8) // 2
            hf = c % 2
            return w.v(w.t[:, j, :].bitcast(BF16)[:, hf * NT:(hf + 1) * NT], (j,))
        c -= 24
        t = SS[16 + c // 2]
        hf = c % 2
        return t.v(t.t[:].bitcast(BF16)[:, hf * NT:(hf + 1) * NT], (hf,))

    def layer(l, ti):
        if l == 0:
            prenorm(l, "nmp")
        for nm, fn, slabs, m0 in (("A", mixer_A, (0, 1, 2, 3), 0), ("B", mixer_B, (4, 5, 6, 7), 4),
                                  ("C", mixer_C, (8, 9), 8), ("D", mixer_D, (10, 11), 12)):
            if nm in mixers:
                fn(l)
            else:
                for s_ in slabs:
                    next_slab(l, s_)
                P.memset(MIXT.cs(m0, m0 + 4), 0.0)
        if dbg == ("mix", l) and dbg_d is not None:
            for c in range(16):
                t = salloc()
                P.copy(t.all(), MIXT.c(c))
                P.dma("sp", dbg_d[c * 128:(c + 1) * 128, ti * NT:(ti + 1) * NT], t.all(), "out")
                sfree(t)
        assert len(free) == 28, len(free)
        Y = [SS[i] for i in range(16)]
        free[:] = [i for i in free if i >= 16]
        sqs = salloc()
        for nb in range(4):
            sl = next_slab(l, 12 + nb)
            for j in range(4):
                pz = ps()
                proj_F(sl, j, lambda k: MIXT.c(k), pz.all())
                evac_y(l, "nmo", nb * 4 + j, pz, Y[nb * 4 + j], sqs)
        post_pre(Y, sqs, l, "nfp")
        sfree(sqs)
        for sb in range(16):
            sl = next_slab(l, 16 + sb)
            for j in range(4):
                pz = ps()
                proj_F(sl, j, hT_chunk, pz.all())
                r = CB[j % 2]
                P.act(r.v(r.t[:, 0:NT]), pz.all(), AF.Relu)
                P.tt(hid(sb * 4 + j), r.v(r.t[:, 0:NT]), r.v(r.t[:, 0:NT]), ALU.mult)
        sqs2 = CBQ
        for nb in range(4):
            pz = [ps() for _ in range(4)]
            for g in range(4):
                sl = next_slab(l, 32 + nb * 4 + g)
                for j in range(4):
                    proj_F(sl, j, lambda k: hid(g * 16 + k), pz[j].all(), first=(g == 0), last=(g == 3))
            for j in range(4):
                evac_y(l, "nfo", nb * 4 + j, pz[j], Y[nb * 4 + j], sqs2)
        if l + 1 < depth:
            post_pre(Y, sqs2, l + 1, "nmp")
        else:
            post_pre(Y, sqs2, None, None)
        free[:] = list(range(28))

    for ti in range(ntile):
        P.dma("sp", X.all(), xT_d.rearrange("(c p) t -> p c t", p=128)[:, :, ti * NT:(ti + 1) * NT], "x")
        for l in range(depth):
            layer(l, ti)
        P.dma("sp", yT_d.rearrange("(c p) t -> p c t", p=128)[:, :, ti * NT:(ti + 1) * NT], X.all(), "out")
    P.finalize(final_waits=("out",))
    return nc, P


def kernel(**inp):
    inp = {k: np.asarray(v) for k, v in inp.items()}
    x = inp["x"]
    B, T, _ = x.shape
    depth = inp["w_in"].shape[0]
    nc, _ = build(T, depth)
    wsl, wsm = pack_weights(inp, depth)
    ppk = pack_params(inp, depth)
    cst = make_consts()
    gws = np.ascontiguousarray(inp["gmlp_w_s"].transpose(0, 3, 1, 2).reshape(depth, 128, 512))
    gbs = np.ascontiguousarray(np.broadcast_to(inp["gmlp_b_s"].reshape(depth, 1, 512), (depth, 128, 512)))
    ncore = 8
    active = [0, 1, 4, 5]
    zeros = {"xT": np.zeros((DM, T), np.float32), "wsl": np.zeros_like(wsl), "wsm": np.zeros_like(wsm),
             "pp": np.zeros_like(ppk), "cst": cst, "gws": np.zeros_like(gws), "gbs": np.zeros_like(gbs)}
    in_maps = [zeros] * ncore
    for b in range(B):
        in_maps[active[b]] = {"xT": np.ascontiguousarray(x[b].T), "wsl": wsl, "wsm": wsm, "pp": ppk, "cst": cst,
                              "gws": gws, "gbs": gbs}
    res = run_bass_kernel_spmd(nc, in_maps, core_ids=list(range(ncore)))
    out = np.empty_like(x)
    for b in range(B):
        out[b] = res.results[active[b]]["yT"].T
    return out
```
